# Optimizing a Trainium2 kernel written in Bass

```python
import math
import jax, jax.numpy as jnp
from jax import lax
import numpy as np

D_MODEL = 4096
BATCH = 16
SEQ = 256
DEPTH = 2
DEC_BATCH = 2
DEC_SEQ = 2048
PAST_LEN = 512

GRID_W = 64
N_MIXERS = 2
N_HY_LAYERS = (DEPTH + 1) // 2
N_RET_LAYERS = DEPTH // 2
HY_EMB = 33
HY_BANDS = (HY_EMB - 1) // 2
HY_ORDER = 64
HY_FAST_DECAY = 0.3
HY_SLOW_DECAY = 1.5
HY_TARGET = 1e-2
RET_HEADS = 16
RET_DK = D_MODEL // RET_HEADS
RET_DV = 2 * D_MODEL // RET_HEADS
RET_CHUNK = 128
ROPE_THETA = 10000.0
D_FF = 11008
EPS = 1e-6

kernel_name = "hyena_retention_flow_backbone_step"

F32 = jnp.float32


def _rmsnorm(x, w):
    x32 = x.astype(F32)
    y = x32 * lax.rsqrt(jnp.mean(x32 * x32, axis=-1, keepdims=True) + EPS)
    return (y * w.astype(F32)).astype(x.dtype)


def _modulate(x, w, shift, scale):
    return _rmsnorm(x, w) * (1 + scale) + shift


def _dwconv3(x, w):
    xp = jnp.pad(x, ((0, 0), (1, 1), (0, 0)))
    return xp[:, :-2] * w[0] + xp[:, 1:-1] * w[1] + xp[:, 2:] * w[2]


def _hyena_filter(L, w1, b1, w2, b2, w3, b3, freq, w_out):
    t = jnp.linspace(0.0, 1.0, L, dtype=F32)[:, None]
    w = 2.0 * math.pi * jnp.arange(L, dtype=F32)[:, None] / L
    f = jnp.linspace(1e-4, HY_BANDS - 1, HY_BANDS, dtype=F32)[None, :]
    z = jnp.concatenate([t, jnp.cos(f * w), -jnp.sin(f * w)], axis=-1)
    h = jnp.sin(freq[0] * (z @ w1 + b1))
    h = jnp.sin(freq[1] * (h @ w2 + b2))
    h = jnp.sin(freq[2] * (h @ w3 + b3))
    h = (h @ w_out).astype(F32)
    deltas = jnp.abs(jnp.linspace(math.log(HY_FAST_DECAY) / HY_TARGET,
                                  math.log(HY_SLOW_DECAY) / HY_TARGET, D_MODEL, dtype=F32))
    decay = jnp.exp(-t * deltas)
    h_fwd = h[:, :D_MODEL] * decay
    h_bwd = h[:, D_MODEL:] * decay
    return jnp.concatenate([h_fwd, jnp.zeros((1, D_MODEL), F32), h_bwd[:0:-1]], axis=0)


def _hyena_mixer(h, w_in, short_w, short_b, fw1, fb1, fw2, fb2, fw3, fb3, ffreq, fwout, fbias, w_out):
    B, L, _ = h.shape
    u = _dwconv3(h @ w_in, short_w) + short_b
    x0, x1, v = jnp.split(u, 3, axis=-1)
    z = (v * x1).astype(F32)
    k = _hyena_filter(L, fw1, fb1, fw2, fb2, fw3, fb3, ffreq, fwout)
    n = 2 * L
    y = jnp.fft.irfft(jnp.fft.rfft(z, n=n, axis=1) * jnp.fft.rfft(k, axis=0)[None], n=n, axis=1)[:, :L]
    y = (y + z * fbias.astype(F32)) * x0.astype(F32)
    return y.astype(h.dtype) @ w_out


def _rope_tables(L):
    rows = L // GRID_W
    row = jnp.repeat(jnp.arange(rows, dtype=F32), GRID_W)
    col = jnp.tile(jnp.arange(GRID_W, dtype=F32), rows)
    n_freq = RET_DK // 4
    inv = ROPE_THETA ** (-jnp.arange(n_freq, dtype=F32) / n_freq)
    ang = jnp.concatenate([row[:, None] * inv, col[:, None] * inv], axis=-1)
    return jnp.cos(ang), jnp.sin(ang)


def _apply_rope(x, cos, sin):
    x1, x2 = x[..., :RET_DK // 2], x[..., RET_DK // 2:]
    return jnp.concatenate([x1 * cos - x2 * sin, x1 * sin + x2 * cos], axis=-1)


def _retention_scan(q, k, v, log_gamma, s0):
    B, H, L, _ = q.shape
    n = L // RET_CHUNK
    idx = jnp.arange(RET_CHUNK, dtype=F32)
    log_g = log_gamma[:, None]
    diff = idx[:, None] - idx[None, :]
    decay_mask = jnp.where(diff >= 0, jnp.exp(log_g[:, :, None] * jnp.maximum(diff, 0.0)), 0.0)
    q_dec = jnp.exp(log_g * (idx + 1.0))[..., None]
    k_dec = jnp.exp(log_g * (RET_CHUNK - 1.0 - idx))[..., None]
    chunk_dec = jnp.exp(log_gamma * RET_CHUNK)[:, None, None]

    def to_chunks(t):
        return jnp.moveaxis(t.reshape(B, H, n, RET_CHUNK, t.shape[-1]), 2, 0)

    def step(s, inp):
        qc, kc, vc = inp
        scores = jnp.einsum('bhid,bhjd->bhij', qc, kc) * decay_mask
        inner = jnp.einsum('bhij,bhjv->bhiv', scores, vc)
        cross = jnp.einsum('bhid,bhdv->bhiv', qc * q_dec, s)
        s_new = s * chunk_dec + jnp.einsum('bhjd,bhjv->bhdv', kc * k_dec, vc)
        return s_new, inner + cross

    s_final, out = lax.scan(step, s0, (to_chunks(q), to_chunks(k), to_chunks(v)))
    out = jnp.moveaxis(out, 0, 2).reshape(B, H, L, v.shape[-1])
    return out, s_final


def _retention_mixer(h, w_in, log_gamma, gn_w, w_out, s0_fwd, s0_bwd, rope):
    B, L, _ = h.shape
    proj = h @ w_in
    q, k, v, g = jnp.split(proj, [D_MODEL, 2 * D_MODEL, 4 * D_MODEL], axis=-1)

    def heads(t, d):
        return t.reshape(B, L, RET_HEADS, d).transpose(0, 2, 1, 3).astype(F32)

    q = heads(q, RET_DK)
    k = heads(k, RET_DK) * (RET_DK ** -0.5)
    v = heads(v, RET_DV)
    if rope is not None:
        q = _apply_rope(q, *rope)
        k = _apply_rope(k, *rope)
    o_f, s_f = _retention_scan(q, k, v, log_gamma[0], s0_fwd)
    o_b, s_b = _retention_scan(jnp.flip(q, 2), jnp.flip(k, 2), jnp.flip(v, 2), log_gamma[1], s0_bwd)
    o = o_f + jnp.flip(o_b, 2)
    o = o * lax.rsqrt(jnp.mean(o * o, axis=-1, keepdims=True) + EPS)
    o = o.transpose(0, 2, 1, 3).reshape(B, L, 2 * D_MODEL) * gn_w.astype(F32)
    out = (jax.nn.silu(g.astype(F32)) * o).astype(h.dtype) @ w_out
    return out, s_f, s_b


def _conv_ffn(h, w_up, conv_w, w_down):
    u = _dwconv3(h @ w_up, conv_w)
    a, b = jnp.split(u, 2, axis=-1)
    return (jax.nn.silu(a) * b) @ w_down


def setup_inputs(seed: int = 0) -> dict:
    key = jax.random.key(seed)
    ks = jax.random.split(key, 32)
    D, F, H = D_MODEL, D_FF, RET_HEADS

    def nrm(k, shape, s):
        return jax.random.normal(k, shape, jnp.float32) * s

    gamma0 = 1.0 - jnp.exp(jnp.linspace(math.log(1.0 / 32), math.log(1.0 / 512), H, dtype=jnp.float32))
    logit0 = jnp.log(gamma0) - jnp.log1p(-gamma0)
    return {
        "x_prompt": nrm(ks[0], (BATCH, SEQ, D), 1.0),
        "x_sample": nrm(ks[1], (DEC_BATCH, DEC_SEQ, D), 1.0),
        "state_retention": nrm(ks[2], (DEC_BATCH, N_RET_LAYERS, 2, H, RET_DK, RET_DV), 0.5),
        "c": nrm(ks[3], (DEC_BATCH, D), 1.0),
        "c_ctx": nrm(ks[4], (D,), 1.0),
        "ada_w": nrm(ks[5], (DEPTH, D, 6 * D), 0.5 * D ** -0.5),
        "ada_b": nrm(ks[6], (DEPTH, 6 * D), 0.02),
        "norm1_w": 1.0 + nrm(ks[7], (DEPTH, D), 0.02),
        "norm2_w": 1.0 + nrm(ks[8], (DEPTH, D), 0.02),
        "hy_w_in": nrm(ks[9], (N_HY_LAYERS, D, 3 * D), D ** -0.5),
        "hy_short_w": nrm(ks[10], (N_HY_LAYERS, 3, 3 * D), 3 ** -0.5),
        "hy_short_b": nrm(ks[11], (N_HY_LAYERS, 3 * D), 0.02),
        "hy_f_w1": nrm(ks[12], (N_HY_LAYERS, HY_EMB, HY_ORDER), HY_EMB ** -0.5),
        "hy_f_b1": nrm(ks[13], (N_HY_LAYERS, HY_ORDER), 0.1),
        "hy_f_w2": nrm(ks[14], (N_HY_LAYERS, HY_ORDER, HY_ORDER), HY_ORDER ** -0.5),
        "hy_f_b2": nrm(ks[15], (N_HY_LAYERS, HY_ORDER), 0.1),
        "hy_f_w3": nrm(ks[16], (N_HY_LAYERS, HY_ORDER, HY_ORDER), HY_ORDER ** -0.5),
        "hy_f_b3": nrm(ks[17], (N_HY_LAYERS, HY_ORDER), 0.1),
        "hy_f_freq": 1.0 + nrm(ks[18], (N_HY_LAYERS, 3, HY_ORDER), 0.02),
        "hy_f_wout": nrm(ks[19], (N_HY_LAYERS, HY_ORDER, 2 * D), 0.2 * HY_ORDER ** -0.5),
        "hy_f_bias": nrm(ks[20], (N_HY_LAYERS, D), 0.5),
        "hy_w_out": nrm(ks[21], (N_HY_LAYERS, D, D), D ** -0.5),
        "ret_w_in": nrm(ks[22], (N_RET_LAYERS, D, 6 * D), D ** -0.5),
        "ret_decay_logit": logit0 + nrm(ks[23], (N_RET_LAYERS, 2, H), 0.05),
        "ret_gn_w": 1.0 + nrm(ks[24], (N_RET_LAYERS, 2 * D), 0.02),
        "ret_w_out": nrm(ks[25], (N_RET_LAYERS, 2 * D, D), (2 * D) ** -0.5),
        "ffn_w_up": nrm(ks[26], (DEPTH, D, 2 * F), D ** -0.5),
        "ffn_conv_w": nrm(ks[27], (DEPTH, 3, 2 * F), 3 ** -0.5),
        "ffn_w_down": nrm(ks[28], (DEPTH, F, D), F ** -0.5),
        "final_norm_w": 1.0 + nrm(ks[29], (D,), 0.02),
    }


def reference(x_prompt, x_sample, state_retention, c, c_ctx, ada_w, ada_b, norm1_w, norm2_w,
              hy_w_in, hy_short_w, hy_short_b, hy_f_w1, hy_f_b1, hy_f_w2, hy_f_b2, hy_f_w3, hy_f_b3,
              hy_f_freq, hy_f_wout, hy_f_bias, hy_w_out, ret_w_in, ret_decay_logit, ret_gn_w, ret_w_out,
              ffn_w_up, ffn_conv_w, ffn_w_down, final_norm_w):
    xp, xs = x_prompt, x_sample
    Bp = xp.shape[0]
    rope_lat = _rope_tables(xs.shape[1])
    new_states = []
    for i in range(DEPTH):
        j = i // N_MIXERS
        mod_p = (jax.nn.silu(c_ctx) @ ada_w[i] + ada_b[i])[None, None, :]
        mod_s = (jax.nn.silu(c) @ ada_w[i] + ada_b[i])[:, None, :]
        sh1p, sc1p, g1p, sh2p, sc2p, g2p = jnp.split(mod_p, 6, axis=-1)
        sh1s, sc1s, g1s, sh2s, sc2s, g2s = jnp.split(mod_s, 6, axis=-1)
        hp = _modulate(xp, norm1_w[i], sh1p, sc1p)
        hs = _modulate(xs, norm1_w[i], sh1s, sc1s)
        if i % N_MIXERS == 0:
            hy = (hy_w_in[j], hy_short_w[j], hy_short_b[j], hy_f_w1[j], hy_f_b1[j], hy_f_w2[j], hy_f_b2[j],
                  hy_f_w3[j], hy_f_b3[j], hy_f_freq[j], hy_f_wout[j], hy_f_bias[j], hy_w_out[j])
            mp = _hyena_mixer(hp, *hy)
            ms = _hyena_mixer(hs, *hy)
        else:
            log_gamma = jax.nn.log_sigmoid(ret_decay_logit[j].astype(F32))
            zeros = jnp.zeros((Bp, RET_HEADS, RET_DK, RET_DV), F32)
            mp, s_f, s_b = _retention_mixer(hp, ret_w_in[j], log_gamma, ret_gn_w[j], ret_w_out[j],
                                            zeros, zeros, None)
            ms, _, _ = _retention_mixer(hs, ret_w_in[j], log_gamma, ret_gn_w[j], ret_w_out[j],
                                        state_retention[:, j, 0].astype(F32),
                                        state_retention[:, j, 1].astype(F32), rope_lat)
            new_states.append(jnp.stack([s_f, s_b], axis=1))
        xp = xp + g1p * mp
        xs = xs + g1s * ms
        hp = _modulate(xp, norm2_w[i], sh2p, sc2p)
        hs = _modulate(xs, norm2_w[i], sh2s, sc2s)
        xp = xp + g2p * _conv_ffn(hp, ffn_w_up[i], ffn_conv_w[i], ffn_w_down[i])
        xs = xs + g2s * _conv_ffn(hs, ffn_w_up[i], ffn_conv_w[i], ffn_w_down[i])
    y_prompt = _rmsnorm(xp, final_norm_w)
    y_sample = _rmsnorm(xs, final_norm_w)
    new_state_retention = jnp.stack(new_states, axis=1)
    return (y_prompt, y_sample, new_state_retention)
```

```python
import concourse.bass as bass
import concourse.mybir as mybir

SEM_CAP = 24000
DMA_SLOTS = 6
COMPUTE = ("pe", "act", "dve", "pool")
QUEUES = ("sp", "pq")


class Op:
    __slots__ = ("eng", "fn", "deps", "raw", "sig", "idx", "dma_slot", "dma_val", "is_dma", "nsig")

    def __init__(self, eng, fn, is_dma):
        self.eng = eng
        self.fn = fn
        self.deps = set()
        self.raw = set()
        self.sig = False
        self.is_dma = is_dma
        self.dma_slot = None
        self.dma_val = None
        self.nsig = None


class Sched:
    def __init__(self, nc):
        self.nc = nc
        self.ops = []
        self.last_w = {}
        self.readers = {}
        self.bar_deps = set()
        self.last_on = {}

    def add(self, eng, fn, reads=(), writes=()):
        is_dma = eng in QUEUES
        op = Op(eng, fn, is_dma)
        i = len(self.ops)
        op.idx = i
        for k in reads:
            w = self.last_w.get(k)
            if w is not None:
                op.deps.add(w)
                op.raw.add(w)
            self.readers.setdefault(k, []).append(i)
        for k in writes:
            w = self.last_w.get(k)
            if w is not None:
                op.deps.add(w)
            for r in self.readers.get(k, ()):
                if r != i:
                    op.deps.add(r)
            self.last_w[k] = i
            self.readers[k] = []
        op.deps |= self.bar_deps
        if is_dma:
            self.last_on.setdefault(eng, []).append(i)
            if len(self.last_on[eng]) > DMA_SLOTS:
                self.last_on[eng] = self.last_on[eng][-DMA_SLOTS:]
        else:
            self.last_on[eng] = i
        self.ops.append(op)
        return i

    def barrier(self):
        deps = set()
        for e, v in self.last_on.items():
            if isinstance(v, list):
                deps.update(v)
            else:
                deps.add(v)
        self.bar_deps = deps

    def emit(self):
        nc = self.nc
        ops = self.ops
        stream_of = {"pe": "pe", "act": "act", "dve": "dve", "pool": "pool", "sp": "sp", "pq": "pool"}
        for op in ops:
            keep = set()
            best = {}
            for d in op.deps:
                p = ops[d]
                if p.is_dma:
                    keep.add(d)
                    continue
                if p.eng == op.eng and not op.is_dma:
                    if p.eng == "pe":
                        continue
                    if d not in op.raw:
                        continue
                if best.get(p.eng, -1) < d:
                    best[p.eng] = d
            keep.update(best.values())
            op.deps = keep
        qcount = {q: 0 for q in QUEUES}
        qhist = {q: [] for q in QUEUES}
        for op in ops:
            if op.is_dma:
                n = qcount[op.eng]
                op.dma_slot = n % DMA_SLOTS
                op.dma_val = 16 * (n // DMA_SLOTS + 1)
                if n >= DMA_SLOTS:
                    op.deps.add(qhist[op.eng][n - DMA_SLOTS])
                qhist[op.eng].append(op.idx)
                qcount[op.eng] = n + 1
        for op in ops:
            for d in op.deps:
                ops[d].sig = True
        cnt = {e: 0 for e in COMPUTE}
        for op in ops:
            if not op.is_dma and op.sig:
                op.nsig = cnt[op.eng]
                cnt[op.eng] += 1
        import contextlib
        es = contextlib.ExitStack()
        sems = {}
        for e in COMPUTE:
            n = (cnt[e] + SEM_CAP - 1) // SEM_CAP
            sems[e] = [es.enter_context(nc.semaphore(f"s_{e}{j}")) for j in range(max(n, 1))]
        dsem = {q: [es.enter_context(nc.semaphore(f"d_{q}{j}")) for j in range(DMA_SLOTS)] for q in QUEUES}
        self.nsems = sum(len(v) for v in sems.values()) + 2 * DMA_SLOTS
        streams = {"pe": [], "act": [], "dve": [], "pool": [], "sp": []}
        for op in ops:
            streams[stream_of[op.eng]].append(op)

        def run_stream(engobj, lst):
            waited = {}
            for op in lst:
                for d in sorted(op.deps):
                    p = ops[d]
                    if p.is_dma:
                        s = dsem[p.eng][p.dma_slot]
                        v = p.dma_val
                    else:
                        s = sems[p.eng][p.nsig // SEM_CAP]
                        v = p.nsig % SEM_CAP + 1
                    key = id(s)
                    if waited.get(key, 0) >= v:
                        continue
                    waited[key] = v
                    engobj.wait_ge(s, v)
                ins = op.fn(engobj)
                if op.is_dma:
                    ins.then_inc(dsem[op.eng][op.dma_slot], 16)
                elif op.sig:
                    ins.then_inc(sems[op.eng][op.nsig // SEM_CAP], 1)
            return

        with nc.Block() as block:
            @block.tensor
            def _(e):
                run_stream(e, streams["pe"])

            @block.scalar
            def _(e):
                run_stream(e, streams["act"])

            @block.vector
            def _(e):
                run_stream(e, streams["dve"])

            @block.gpsimd
            def _(e):
                run_stream(e, streams["pool"])

            @block.sync
            def _(e):
                run_stream(e, streams["sp"])
                for q in QUEUES:
                    n = qcount[q]
                    for sl in range(DMA_SLOTS):
                        k = (n - sl + DMA_SLOTS - 1) // DMA_SLOTS
                        if k > 0:
                            e.wait_ge(dsem[q][sl], 16 * k)
        es.close()

import contextlib
import math
import numpy as np
import ml_dtypes
from concourse.bass_utils import run_bass_kernel_spmd

F32 = mybir.dt.float32
BF16 = mybir.dt.bfloat16
AF = mybir.ActivationFunctionType
ALU = mybir.AluOpType
EPS = 1e-6


class Cfg:
    def __init__(self, D, NH, DFF, TS, LP, BATCH, NE):
        self.D, self.NH, self.DFF, self.TS, self.LP, self.BATCH, self.NE = D, NH, DFF, TS, LP, BATCH, NE
        self.NCD = D // 128
        self.NFF = DFF // 128
        self.TP = TS // 2
        self.NG = TS // 128
        self.GP = self.TP // 128
        self.DK, self.DV = 256, 512
        assert D == NH * 256
        self.NPC = TS // LP
        self.NCORES = 2 + BATCH // self.NPC
        self.CPS = LP // 128
        self.NFIN = self.NG // self.CPS
        self.NCB = D // 512
        base, rem = divmod(self.NFF, NE)
        self.ESZ = [base + (1 if i < rem else 0) for i in range(NE)]


FULL = Cfg(D=4096, NH=16, DFF=11008, TS=2048, LP=256, BATCH=16, NE=8)


def pieces(n, mx=512):
    out, c = [], 0
    while c < n:
        m = min(mx, n - c)
        out.append((c, m))
        c += m
    return out


def build(cfg):
    nc = bass.Bass("TRN2", target_bir_lowering=False)
    S = Sched(nc)
    D, NCD, NH, TS, TP, NG, GP, NFF, NCB = cfg.D, cfg.NCD, cfg.NH, cfg.TS, cfg.TP, cfg.NG, cfg.GP, cfg.NFF, cfg.NCB
    NCC = 2 * NG
    names_in = []

    def din(name, shape, dt=F32):
        names_in.append(name)
        return nc.dram_tensor(name, list(shape), dt, kind="ExternalInput").ap()

    def dout(name, shape, dt=F32):
        return nc.dram_tensor(name, list(shape), dt, kind="ExternalOutput").ap()

    def dscr(name, shape, dt=F32):
        return nc.dram_tensor(name, list(shape), dt, kind="Internal").ap()

    xT0 = din("xT0", [D, TS])
    cvec = din("cvec", [128, NCD])
    ada_wb = din("ada_wb", [12 * NCD, 128, NCD * 128])
    ada_bT = din("ada_bT", [128, 12 * NCD])
    n1T = din("n1T", [128, 2 * NCD])
    n2T = din("n2T", [128, 2 * NCD])
    nfT = din("nfT", [128, NCD])
    mS = din("mS", [128, TS])
    mE = din("mE", [128, TS])
    hy_win = din("hy_win", [3 * NCD, 128, NCD * 128])
    hy_cw = din("hy_cw", [128, 3 * 3 * NCD])
    hy_sb = din("hy_sb", [128, 3 * NCD])
    hy_w1 = din("hy_w1", [33, 64]); hy_w2 = din("hy_w2", [64, 64]); hy_w3 = din("hy_w3", [64, 64])
    hy_bf = din("hy_bf", [64, 6])
    hy_wo = din("hy_wo", [64, 2 * D])
    hy_fb = din("hy_fb", [128, D])
    hy_wout = din("hy_wout", [NCD, 128, NCD * 128])
    zembT = din("zembT", [33, TS])
    negt = din("negt", [128, NG]); nfirst = din("nfirst", [128, NG])
    deltab = din("deltab", [128, D])
    nsT = din("nsT", [128, NG]); spT = din("spT", [128, NG])
    Fre = din("Fre", [NG, 128, NG * 128], BF16)
    Fim = din("Fim", [NG, 128, NG * 128], BF16)
    Fzi = din("Fzi", [NG, 128, NG * 128], BF16)
    Fn = din("Fn", [NG, 128, NG * 128], BF16)
    Gm = din("Gm", [NG, 128, NCC * 128], BF16)
    ffn_up = [din(f"ffn_up{l}", [2 * NFF, 128, NCD * 128]) for l in range(2)]
    ffn_cw = [din(f"ffn_cw{l}", [128, 3 * 2 * NFF]) for l in range(2)]
    ffn_dn = [[din(f"ffn_dn{l}_{e}", [NCD, 128, cfg.ESZ[e] * 128]) for e in range(cfg.NE)] for l in range(2)]
    ret_qk = din("ret_qk", [4 * NH, 128, NCD * 128])
    ret_vg = din("ret_vg", [4 * NH, 128, NCD * 256])
    ret_wo = din("ret_wo", [2, NCD, 128, NCD * 128])
    ret_lg = din("ret_lg", [128, 2 * NH])
    ret_gn = din("ret_gn", [128, 2 * D])
    s0 = din("s0", [2, NH, 256, 512])
    cosq = din("cosq", [128, TS]); sinq = din("sinq", [128, TS])
    rtab = din("rtab", [128, 7 * 128])
    kpos = din("kpos", [128, 2 * NH])
    cmf = din("cmf", [128, NG]); cmb = din("cmb", [128, NG])
    identb = din("identb", [128, 128], BF16)

    yT = dout("yT", [D, TS])
    st_out = dout("st_out", [2, NH, cfg.NFIN, 256, 512])

    xT = dscr("xT", [D, TS])
    ztok = dscr("ztok", [TS, D], BF16)
    x0tok = dscr("x0tok", [TS, D], BF16)
    gT = dscr("gT", [2 * D, TS], BF16)
    qT_d = dscr("qT_d", [2 * NH * 128, TS], BF16)
    kT_d = dscr("kT_d", [2 * NH * 128, TS], BF16)
    kf_d = dscr("kf_d", [TS, NH * 256], BF16)
    kb_d = dscr("kb_d", [TS, NH * 256], BF16)
    v_d = dscr("v_d", [TS, NH * 512], BF16)
    sg_d = dscr("sg_d", [TS, NH * 512], BF16)

    es0 = contextlib.ExitStack()
    uniq = [0]

    def sb(name, shape, dt=F32, st=es0):
        uniq[0] += 1
        return st.enter_context(nc.sbuf_tensor(f"{name}_{uniq[0]}", list(shape), dt))
    ps = [es0.enter_context(nc.psum_tensor(f"ps{i}", [128, 512], F32)) for i in range(6)]
    pst = [es0.enter_context(nc.psum_tensor(f"pst{i}", [128, 1024], BF16)) for i in range(2)]
    bank_ctr = [0, 0]

    def nbank():
        b = bank_ctr[0] % 6
        bank_ctr[0] += 1
        return b

    def ntbank():
        b = bank_ctr[1] % 2
        bank_ctr[1] += 1
        return b

    def dma(q, out, in_, r=(), w=()):
        S.add(q, lambda e: e.dma_start(out=out, in_=in_, allow_slow_non_contiguous=True), r, w)

    def act(out, in_, func, r=(), w=(), **kw):
        S.add("act", lambda e: e.activation(out=out, in_=in_, func=func, **kw), r, w)

    def tt(out, in0, in1, op, r=(), w=()):
        S.add("dve", lambda e: e.tensor_tensor(out=out, in0=in0, in1=in1, op=op), r, w)

    def stt(out, in0, scalar, in1, op0, op1, r=(), w=()):
        S.add("dve", lambda e: e.scalar_tensor_tensor(out=out, in0=in0, scalar=scalar, in1=in1, op0=op0, op1=op1), r, w)

    def tsc(out, in0, s1, s2, op0, op1, r=(), w=()):
        S.add("dve", lambda e: e.tensor_scalar(out=out, in0=in0, scalar1=s1, scalar2=s2, op0=op0, op1=op1), r, w)

    def tsc1(out, in0, s1, op0, r=(), w=()):
        S.add("dve", lambda e: e.tensor_scalar(out=out, in0=in0, scalar1=s1, scalar2=None, op0=op0), r, w)

    def rsqrt_(ap, key):
        act(ap, ap, AF.Sqrt, r=[key], w=[key])
        S.add("dve", lambda e: e.reciprocal(out=ap, in_=ap), [key], [key])

    def mm(out, lhsT, rhs, start, stop, r=(), w=()):
        S.add("pe", lambda e: e.matmul(out, lhsT, rhs, start=start, stop=stop), r, w)

    def tr(out, in_, ident, r=(), w=()):
        S.add("pe", lambda e: e.transpose(out, in_, ident), r, w)

    def memset(ap, val, w=()):
        S.add("dve", lambda e: e.memset(ap, val), (), w)

    modT = sb("modT", [128, 12 * NCD])
    wm = sb("wm", [128, 5 * NCD])
    zshift = sb("zshift", [128, NCD])
    ones = sb("ones", [128, 128])
    idb = sb("idb", [128, 128], BF16)
    wbuf = [sb(f"wbuf{i}", [128, NCD * 128], BF16) for i in range(3)]
    wctr = [0]
    memset(ones[:], 1.0, w=["ones"])
    ones_b = sb("ones_b", [128, 128], BF16)
    memset(ones_b[:], 1.0, w=["ones"])
    memset(zshift[:], 0.0, w=["zshift"])
    dma("sp", idb[:], identb[:, :], w=["idb"])
    for c in range(NCD):
        dma("sp", xT[c * 128:(c + 1) * 128, :], xT0[c * 128:(c + 1) * 128, :], w=[("xT", c)])

    def MOD(l, m, c=None):
        base = (l * 6 + m) * NCD
        return modT[:, base:base + NCD] if c is None else modT[:, base + c:base + c + 1]

    def proj_ws(wd, blocks, KC, rhs_pieces, consume, rkeys, prepare=None):
        for bi, blk in enumerate(blocks):
            wi = wctr[0] % 3
            wctr[0] += 1
            wb = wbuf[wi]
            dma("pq", wb[:, 0:KC * 128], wd[blk], w=[("wbuf", wi)])
            bks = [nbank() for _ in rhs_pieces]
            if prepare is not None:
                for pi in range(len(rhs_pieces)):
                    prepare(bi, pi)
            for k in range(KC):
                for pi, (fn, n) in enumerate(rhs_pieces):
                    b = bks[pi]
                    mm(ps[b][:, 0:n], wb[:, k * 128:(k + 1) * 128], fn(k), k == 0, k == KC - 1,
                       r=[("wbuf", wi)] + list(rkeys), w=[("ps", b)])
            for pi, (fn, n) in enumerate(rhs_pieces):
                consume(bi, pi, bks[pi], n)

    def norm_mod(st, col0, ncols, wmod, shift, dst, dkey, out_dram=None):
        xa = sb(f"nm_xa_{dkey}", [128, NCD, 512], F32, st)
        sq = [sb(f"nm_sq{i}_{dkey}", [128, 512], BF16, st) for i in range(4)]
        rstd = sb(f"nm_rstd_{dkey}", [128, 512], F32, st)
        tmp = [sb(f"nm_tmp{i}_{dkey}", [128, 512], F32, st) for i in range(2)]
        ob = [sb(f"nm_ob{i}_{dkey}", [128, 512], F32, st) for i in range(2)] if out_dram is not None else None
        NQ = 4 if NCD % 4 == 0 else 1
        CQ = NCD // NQ
        for (c0, n) in pieces(ncols):
            for qd in range(NQ):
                dma("sp", xa[:, qd * CQ:(qd + 1) * CQ, 0:n],
                    xT[qd * CQ * 128:(qd + 1) * CQ * 128, col0 + c0:col0 + c0 + n].rearrange("(c p) t -> p c t", p=128),
                    r=[("xT", c) for c in range(qd * CQ, (qd + 1) * CQ)], w=[("nmxa", qd)])
            b = nbank()
            for c in range(NCD):
                i4 = c % 4
                act(sq[i4][:, 0:n], xa[:, c, 0:n], AF.Square, r=[("nmxa", c // CQ)], w=[("nmsq", i4)])
                mm(ps[b][:, 0:n], ones_b[:], sq[i4][:, 0:n], c == 0, c == NCD - 1, r=[("nmsq", i4), "ones"], w=[("ps", b)])
            tsc(rstd[:, 0:n], ps[b][:, 0:n], 1.0 / D, EPS, ALU.mult, ALU.add, r=[("ps", b)], w=["nmrstd"])
            rsqrt_(rstd[:, 0:n], "nmrstd")
            for c in range(NCD):
                i2 = c % 2
                tt(tmp[i2][:, 0:n], xa[:, c, 0:n], rstd[:, 0:n], ALU.mult, r=[("nmxa", c // CQ), "nmrstd"], w=[("nmtmp", i2)])
                if out_dram is None:
                    act(dst(c, c0, n), tmp[i2][:, 0:n], AF.Identity, scale=wmod[:, c:c + 1], bias=shift[:, c:c + 1],
                        r=[("nmtmp", i2), "mod"], w=[dkey])
                else:
                    act(ob[i2][:, 0:n], tmp[i2][:, 0:n], AF.Identity, scale=wmod[:, c:c + 1], bias=shift[:, c:c + 1],
                        r=[("nmtmp", i2), "mod"], w=[("nmob", i2)])
                    dma("sp", out_dram[c * 128:(c + 1) * 128, col0 + c0:col0 + c0 + n], ob[i2][:, 0:n], r=[("nmob", i2)], w=[("yT", c)])

    def make_resid(st, gate, col0, tag):
        xr = [sb(f"xr{i}_{tag}", [128, 512], F32, st) for i in range(6)]
        ctr = [0]
        slot = {}

        def prepare(bi, pi):
            j = bi
            c0, n = pieces(TP)[pi]
            i = ctr[0] % 6
            ctr[0] += 1
            slot[(bi, pi)] = i
            dma("sp", xr[i][:, 0:n], xT[j * 128:(j + 1) * 128, col0 + c0:col0 + c0 + n], r=[("xT", j)], w=[("xr", i)])

        def consume(bi, pi, b, n):
            j = bi
            c0 = pieces(TP)[pi][0]
            i = slot.pop((bi, pi))
            dst = xT[j * 128:(j + 1) * 128, col0 + c0:col0 + c0 + n]
            stt(xr[i][:, 0:n], ps[b][:, 0:n], gate[:, j:j + 1], xr[i][:, 0:n], ALU.mult, ALU.add, r=[("ps", b), ("xr", i), "mod"], w=[("xr", i)])
            dma("sp", dst, xr[i][:, 0:n], r=[("xr", i)], w=[("xT", j)])
        return prepare, consume

    with contextlib.ExitStack() as st:
        cv = sb("cv", [128, NCD], F32, st)
        scT = sb("scT", [128, NCD], BF16, st)
        abT = sb("abT", [128, 12 * NCD], F32, st)
        nw = sb("nw", [128, 5 * NCD], F32, st)
        dma("sp", cv[:], cvec[:, :], w=["cv"])
        dma("sp", abT[:], ada_bT[:, :], w=["abT"])
        dma("sp", nw[:, 0:2 * NCD], n1T[:, :], w=["nw"])
        dma("sp", nw[:, 2 * NCD:4 * NCD], n2T[:, :], w=["nw"])
        dma("sp", nw[:, 4 * NCD:5 * NCD], nfT[:, :], w=["nw"])
        act(scT[:], cv[:], AF.Silu, r=["cv"], w=["scT"])

        def cons_ada(bi, pi, b, n):
            tt(modT[:, bi:bi + 1], ps[b][:, 0:1], abT[:, bi:bi + 1], ALU.add, r=[("ps", b), "abT"], w=["mod"])
        proj_ws(ada_wb, list(range(12 * NCD)), NCD, [(lambda k: scT[:, k:k + 1], 1)], cons_ada, ["scT"])
        for l in range(2):
            stt(wm[:, (2 * l) * NCD:(2 * l + 1) * NCD], MOD(l, 1), 1.0, nw[:, l * NCD:(l + 1) * NCD], ALU.add, ALU.mult, r=["mod", "nw"], w=["mod"])
            stt(wm[:, (2 * l + 1) * NCD:(2 * l + 2) * NCD], MOD(l, 4), 1.0, nw[:, (2 + l) * NCD:(3 + l) * NCD], ALU.add, ALU.mult, r=["mod", "nw"], w=["mod"])
        S.add("dve", lambda e: e.tensor_copy(out=wm[:, 4 * NCD:5 * NCD], in_=nw[:, 4 * NCD:5 * NCD]), ["nw"], ["mod"])
    S.barrier()

    if getattr(cfg, "STOP", 99) <= 0:
        S.emit(); es0.close(); return nc, names_in
    def conv3(o, ub, t1, w0, w1, w2, bias, msp, mep, rk, wk):
        if bias is None:
            act(o[:, :], ub[:, 1:TP + 1], AF.Identity, scale=w1, r=rk + ["cw"], w=[wk])
        else:
            act(o[:, :], ub[:, 1:TP + 1], AF.Identity, scale=w1, bias=bias, r=rk + ["cw"], w=[wk])
        tt(t1[:, :], ub[:, 0:TP], msp, ALU.mult, r=rk + ["masks"], w=["cv_t1"])
        stt(o[:, :], t1[:, :], w0, o[:, :], ALU.mult, ALU.add, r=["cv_t1", wk, "cw"], w=[wk])
        tt(t1[:, :], ub[:, 2:TP + 2], mep, ALU.mult, r=rk + ["masks"], w=["cv_t1"])
        stt(o[:, :], t1[:, :], w2, o[:, :], ALU.mult, ALU.add, r=["cv_t1", wk, "cw"], w=[wk])

    def evac_ub(ub, p, ukey):
        pcs = pieces(TP)

        def f(pi, b, n):
            if pi < len(pcs):
                c0 = pcs[pi][0]
                act(ub[:, 1 + c0:1 + c0 + n], ps[b][:, 0:n], AF.Copy, r=[("ps", b)], w=[ukey])
            else:
                col = TP + 1 if p == 0 else 0
                act(ub[:, col:col + 1], ps[b][:, 0:1], AF.Copy, r=[("ps", b)], w=[ukey])
        return f

    def load_masks(st, p):
        msp = sb(f"msp{p}", [128, TP], F32, st)
        mep = sb(f"mep{p}", [128, TP], F32, st)
        dma("sp", msp[:], mS[:, p * TP:(p + 1) * TP], w=["masks"])
        dma("sp", mep[:], mE[:, p * TP:(p + 1) * TP], w=["masks"])
        return msp, mep

    def make_hT(st, p, wmod, shift, tag, hx=None):
        hT = sb(f"hT_{tag}", [128, NCD, TP + 1], BF16, st)
        with contextlib.ExitStack() as st2:
            norm_mod(st2, p * TP, TP, wmod, shift, lambda c, c0, n: hT[:, c, c0:c0 + n], "hT")
        S.barrier()
        if hx is None:
            with contextlib.ExitStack() as st2:
                xcol = TP if p == 0 else TP - 1
                norm_mod(st2, xcol, 1, wmod, shift, lambda c, c0, n: hT[:, c, TP:TP + 1], "hT")
            S.barrier()
        else:
            S.add("dve", lambda e: e.tensor_copy(out=hT[:, :, TP:TP + 1], in_=hx[:, :, p:p + 1]), ["hx"], ["hT"])
        return hT

    def make_hx(st, wmod, shift):
        hx = sb("hx", [128, NCD, 2], BF16, st)
        for p in range(2):
            with contextlib.ExitStack() as st2:
                xcol = TP if p == 0 else TP - 1
                norm_mod(st2, xcol, 1, wmod, shift, lambda c, c0, n, p=p: hx[:, c, p:p + 1], "hx")
            S.barrier()
        return hx

    for p in range(2):
        with contextlib.ExitStack() as st:
            hT = make_hT(st, p, wm[:, 0:NCD], MOD(0, 0), f"h1_{p}")
            msp, mep = load_masks(st, p)
            cw = sb("hy_cw_s", [128, 9 * NCD], F32, st)
            sbv = sb("hy_sb_s", [128, 3 * NCD], F32, st)
            dma("sp", cw[:], hy_cw[:, :], w=["cw"])
            dma("sp", sbv[:], hy_sb[:, :], w=["cw"])
            ub = [[sb(f"ub{i}_{w_}", [128, TP + 2], F32, st) for w_ in range(3)] for i in range(2)]
            oo = [sb(f"oo{w_}", [128, TP], F32, st) for w_ in range(3)]
            t1 = sb("cv_t1", [128, TP], F32, st)
            zb = [[sb(f"zb{i}_{w_}", [128, TP], BF16, st) for w_ in range(2)] for i in range(2)]
            tk = [sb(f"tk{i}", [128, GP, 128], BF16, st) for i in range(2)]
            for i in range(2):
                for w_ in range(3):
                    memset(ub[i][w_][:, :], 0.0, w=[("ub", i, w_)])
            rp = [(lambda k, c0=c0, n=n: hT[:, k, c0:c0 + n], n) for (c0, n) in pieces(TP)] + [(lambda k: hT[:, k, TP:TP + 1], 1)]

            def finish_h1(i, j):
                for which, key in ((0, "x0"), (1, "z")):
                    tb = ntbank()
                    for g in range(GP):
                        tr(pst[tb][:, g * 128:(g + 1) * 128], zb[i][which][:, g * 128:(g + 1) * 128], idb[:], r=[("zb", i, which), "idb"], w=[("pst", tb)])
                    S.add("act", lambda e, tb=tb, which=which: e.activation(out=tk[which][:, :, :], in_=pst[tb][:, 0:GP * 128].rearrange("p (g c) -> p g c", g=GP), func=AF.Copy),
                          [("pst", tb)], [("tk", which)])
                    dst = (x0tok if which == 0 else ztok)[p * TP:(p + 1) * TP, j * 128:(j + 1) * 128].rearrange("(g q) c -> q g c", q=128)
                    dma("sp", dst, tk[which][:, :, :], r=[("tk", which)], w=[key + "tok"])
            pend = None
            for j in range(NCD):
                i = j % 2
                for w_ in range(3):
                    ev = evac_ub(ub[i][w_], p, ("ub", i, w_))
                    proj_ws(hy_win, [j * 3 + w_], NCD, rp, lambda bi, pi, b, n, ev=ev: ev(pi, b, n), ["hT"])
                for w_ in range(3):
                    idx = w_ * NCD + j
                    conv3(oo[w_], ub[i][w_], t1, cw[:, idx:idx + 1], cw[:, 3 * NCD + idx:3 * NCD + idx + 1],
                          cw[:, 6 * NCD + idx:6 * NCD + idx + 1], sbv[:, idx:idx + 1], msp[:, :], mep[:, :],
                          [("ub", i, w_)], ("oo", w_))
                act(zb[i][0][:, :], oo[0][:, :], AF.Copy, r=[("oo", 0)], w=[("zb", i, 0)])
                tt(zb[i][1][:, :], oo[2][:, :], oo[1][:, :], ALU.mult, r=[("oo", 1), ("oo", 2)], w=[("zb", i, 1)])
                if pend is not None:
                    finish_h1(*pend)
                pend = (i, j)
            finish_h1(*pend)
        S.barrier()

    if getattr(cfg, "STOP", 99) <= 1:
        S.emit(); es0.close(); return nc, names_in
    with contextlib.ExitStack() as st:
        w1s = sb("w1s", [33, 64], F32, st); w2s = sb("w2s", [64, 64], F32, st); w3s = sb("w3s", [64, 64], F32, st)
        bfs = sb("bfs", [64, 6], F32, st)
        sc_ = sb("fsc", [64, 12], F32, st)
        wos = sb("wos", [64, 2, 512], F32, st)
        hA = sb("hA", [64, TS], F32, st)
        ngt = sb("ngt", [128, NG], F32, st); nfs = sb("nfs", [128, NG], F32, st)
        nss = sb("nss", [128, NG], F32, st); sps = sb("sps", [128, NG], F32, st)
        stm = contextlib.ExitStack()
        ze = sb("ze", [33, TS], F32, stm)
        hB = sb("hB", [64, TS], F32, stm)
        s2 = sb("s2", [64, 512], F32, stm); s4 = sb("s4", [64, 512], F32, stm)
        for (dst_, src_) in ((w1s, hy_w1), (w2s, hy_w2), (w3s, hy_w3), (bfs, hy_bf), (ze, zembT),
                             (ngt, negt), (nfs, nfirst), (nss, nsT), (sps, spT)):
            dma("sp", dst_[:], src_[:, :], w=["fconst"])
        for i in range(3):
            tt(sc_[:, 4 * i + 1:4 * i + 2], bfs[:, 3 + i:4 + i], bfs[:, i:i + 1], ALU.mult, r=["fconst"], w=["fsc"])
            tsc1(sc_[:, 4 * i + 3:4 * i + 4], sc_[:, 4 * i + 1:4 * i + 2], 0.25, ALU.mult, r=["fsc"], w=["fsc"])
            tsc1(sc_[:, 4 * i + 1:4 * i + 2], sc_[:, 4 * i + 1:4 * i + 2], 0.5, ALU.mult, r=["fsc"], w=["fsc"])
            tsc1(sc_[:, 4 * i:4 * i + 1], bfs[:, 3 + i:4 + i], 0.5, ALU.mult, r=["fconst"], w=["fsc"])
            tsc1(sc_[:, 4 * i + 2:4 * i + 3], bfs[:, 3 + i:4 + i], 0.25, ALU.mult, r=["fconst"], w=["fsc"])
        srcs = [(w1s, ze, 33), (w2s, hA, 64), (w3s, hB, 64)]
        dsts = [hA, hB, hA]
        for i in range(3):
            wS, src, kk = srcs[i]
            dstT = dsts[i]
            for (c0, n) in pieces(TS):
                b = nbank()
                mm(ps[b][0:64, 0:n], wS[0:kk, :], src[0:kk, c0:c0 + n], True, True, r=["fconst", ("hmlp", i)], w=[("ps", b)])
                act(s2[:, 0:n], ps[b][0:64, 0:n], AF.Sin, scale=sc_[:, 4 * i:4 * i + 1], bias=sc_[:, 4 * i + 1:4 * i + 2], r=[("ps", b), "fsc"], w=["s2"])
                act(s4[:, 0:n], ps[b][0:64, 0:n], AF.Sin, scale=sc_[:, 4 * i + 2:4 * i + 3], bias=sc_[:, 4 * i + 3:4 * i + 4], r=[("ps", b), "fsc"], w=["s4"])
                tt(s4[:, 0:n], s4[:, 0:n], s4[:, 0:n], ALU.mult, r=["s4"], w=["s4"])
                tsc(s4[:, 0:n], s4[:, 0:n], -2.0, 1.0, ALU.mult, ALU.add, r=["s4"], w=["s4"])
                stt(dstT[:, c0:c0 + n], s2[:, 0:n], 2.0, s4[:, 0:n], ALU.mult, ALU.mult, r=["s2", "s4"], w=[("hmlp", i + 1)])
        h3T = hA
        stm.close()
        S.barrier()
        hsum = sb("hsum", [128, NG, 512], BF16, st); hdif = sb("hdif", [128, NG, 512], BF16, st)
        zs = sb("zs", [128, NG, 512], BF16, st); x0g = [sb(f"x0g{i}", [128, 512], BF16, st) for i in range(2)]
        Yb = sb("Yb", [128, NCC, 512], BF16, st)
        dlb = sb("dlb", [128, 512], F32, st); fbb = sb("fbb", [128, 512], F32, st)
        dec = sb("dec", [128, 512], F32, st); hf_ = sb("hf_", [128, 512], F32, st); hb_ = sb("hb_", [128, 512], F32, st)
        Fb = [[sb(f"Fb{i}_{k}", [128, NG * 128], BF16, st) for k in range(4)] for i in range(2)]
        KA = sb("KA", [128, 512], F32, st); KB = sb("KB", [128, 512], F32, st); KC = sb("KC", [128, 512], F32, st)
        c1 = sb("c1", [128, 512], F32, st); c2 = sb("c2", [128, 512], F32, st)
        Gb = [sb(f"Gb{i}", [128, NCC * 128], BF16, st) for i in range(2)]
        gt = sb("gt", [128, 512], F32, st); gbf = sb("gbf", [128, 512], BF16, st)
        gtr = [sb(f"gtr{i}", [128, 4, 128], BF16, st) for i in range(2)]
        for cb in range(NCB):
            cs = slice(cb * 512, (cb + 1) * 512)
            dma("sp", dlb[:], deltab[:, cs], w=["dlb"])
            dma("sp", fbb[:], hy_fb[:, cs], w=["fbb"])
            dma("sp", zs[:, :, :], ztok[:, cs].rearrange("(g q) c -> q g c", q=128), r=["ztok"], w=["zs"])
            dma("sp", wos[:, 0, :], hy_wo[:, cb * 512:(cb + 1) * 512], w=["wos"])
            dma("sp", wos[:, 1, :], hy_wo[:, D + cb * 512:D + (cb + 1) * 512], w=["wos"])
            for jc in range(NG):
                bf_, bb_ = nbank(), nbank()
                mm(ps[bf_][:, :], h3T[:, jc * 128:(jc + 1) * 128], wos[:, 0, :], True, True, r=[("hmlp", 3), "wos"], w=[("ps", bf_)])
                mm(ps[bb_][:, :], h3T[:, jc * 128:(jc + 1) * 128], wos[:, 1, :], True, True, r=[("hmlp", 3), "wos"], w=[("ps", bb_)])
                act(dec[:], dlb[:], AF.Exp, scale=ngt[:, jc:jc + 1], r=["dlb", "fconst"], w=["dec"])
                tt(hf_[:], ps[bf_][:, :], dec[:], ALU.mult, r=[("ps", bf_), "dec"], w=["hf_"])
                stt(hb_[:], ps[bb_][:, :], nfs[:, jc:jc + 1], dec[:], ALU.mult, ALU.mult, r=[("ps", bb_), "dec", "fconst"], w=["hb_"])
                tt(hsum[:, jc, :], hf_[:], hb_[:], ALU.add, r=["hf_", "hb_"], w=["hsum"])
                tt(hdif[:, jc, :], hf_[:], hb_[:], ALU.subtract, r=["hf_", "hb_"], w=["hdif"])
            for i in range(NG):
                fi = i % 2
                special = (i % cfg.CPS == 0)
                dma("sp", Fb[fi][0][:], Fre[i], w=[("Fb", fi, 0)])
                dma("sp", Fb[fi][1][:], Fim[i], w=[("Fb", fi, 1)])
                dma("sp", Fb[fi][2][:], Fzi[i], w=[("Fb", fi, 2)])
                if special:
                    dma("sp", Fb[fi][3][:], Fn[i], w=[("Fb", fi, 3)])
                bkr, bki, bzr, bzi = nbank(), nbank(), nbank(), nbank()
                for jc in range(NG):
                    mm(ps[bkr][:, :], Fb[fi][0][:, jc * 128:(jc + 1) * 128], hsum[:, jc, :], jc == 0, jc == NG - 1, r=[("Fb", fi, 0), "hsum"], w=[("ps", bkr)])
                for jc in range(NG):
                    mm(ps[bki][:, :], Fb[fi][1][:, jc * 128:(jc + 1) * 128], hdif[:, jc, :], jc == 0, (jc == NG - 1) and not special, r=[("Fb", fi, 1), "hdif"], w=[("ps", bki)])
                if special:
                    for jc in range(NG):
                        mm(ps[bki][:, :], Fb[fi][3][:, jc * 128:(jc + 1) * 128], hsum[:, jc, :], False, jc == NG - 1, r=[("Fb", fi, 3), "hsum"], w=[("ps", bki)])
                for jc in range(NG):
                    mm(ps[bzr][:, :], Fb[fi][0][:, jc * 128:(jc + 1) * 128], zs[:, jc, :], jc == 0, jc == NG - 1, r=[("Fb", fi, 0), "zs"], w=[("ps", bzr)])
                for jc in range(NG):
                    mm(ps[bzi][:, :], Fb[fi][2][:, jc * 128:(jc + 1) * 128], zs[:, jc, :], jc == 0, jc == NG - 1, r=[("Fb", fi, 2), "zs"], w=[("ps", bzi)])
                act(KA[:], ps[bkr][:, :], AF.Copy, r=[("ps", bkr)], w=["KA"])
                act(KB[:], ps[bki][:, :], AF.Identity, scale=nss[:, i:i + 1], r=[("ps", bki), "fconst"], w=["KB"])
                act(c1[:], ps[bki][:, :], AF.Identity, scale=sps[:, i:i + 1], r=[("ps", bki), "fconst"], w=["c1"])
                stt(KC[:], KA[:], nss[:, i:i + 1], c1[:], ALU.mult, ALU.add, r=["KA", "c1", "fconst"], w=["KC"])
                tt(c1[:], ps[bzr][:, :], KA[:], ALU.mult, r=[("ps", bzr), "KA", "c1"], w=["c1"])
                tt(c2[:], ps[bzi][:, :], KB[:], ALU.mult, r=[("ps", bzi), "KB"], w=["c2"])
                tt(Yb[:, i, :], c1[:], c2[:], ALU.subtract, r=["c1", "c2"], w=["Yb"])
                tt(c1[:], ps[bzr][:, :], KB[:], ALU.mult, r=[("ps", bzr), "KB"], w=["c1"])
                tt(c2[:], ps[bzi][:, :], KC[:], ALU.mult, r=[("ps", bzi), "KC"], w=["c2"])
                tt(Yb[:, NG + i, :], c1[:], c2[:], ALU.add, r=["c1", "c2"], w=["Yb"])
            for g in range(NG):
                gi = g % 2
                dma("sp", Gb[gi][:], Gm[g], w=[("Gb", gi)])
                b = nbank()
                for i in range(NCC):
                    mm(ps[b][:, :], Gb[gi][:, i * 128:(i + 1) * 128], Yb[:, i, :], i == 0, i == NCC - 1, r=[("Gb", gi), "Yb"], w=[("ps", b)])
                tt(gt[:], zs[:, g, :], fbb[:], ALU.mult, r=["zs", "fbb"], w=["gt"])
                tt(gt[:], ps[b][:, :], gt[:], ALU.add, r=[("ps", b), "gt"], w=["gt"])
                dma("sp", x0g[gi][:], x0tok[g * 128:(g + 1) * 128, cs], r=["x0tok"], w=[("x0g", gi)])
                tt(gbf[:], gt[:], x0g[gi][:], ALU.mult, r=["gt", ("x0g", gi)], w=["gbf"])
                tb = ntbank()
                for cc in range(4):
                    tr(pst[tb][:, cc * 128:(cc + 1) * 128], gbf[:, cc * 128:(cc + 1) * 128], idb[:], r=["gbf", "idb"], w=[("pst", tb)])
                S.add("act", lambda e, tb=tb, gi=gi: e.activation(out=gtr[gi][:, :, :], in_=pst[tb][:, 0:512].rearrange("p (c t) -> p c t", c=4), func=AF.Copy),
                      [("pst", tb)], [("gtr", gi)])
                dma("sp", gT[cb * 512:(cb + 1) * 512, g * 128:(g + 1) * 128].rearrange("(c q) t -> q c t", q=128), gtr[gi][:, :, :], r=[("gtr", gi)], w=["gT"])
    S.barrier()

    if getattr(cfg, "STOP", 99) <= 2:
        S.emit(); es0.close(); return nc, names_in
    def out_proj(wd, nhalf, gate, tag):
        for p in range(2):
            for hf in range(nhalf):
                with contextlib.ExitStack() as st:
                    aT = sb(f"aT_{tag}", [128, NCD, TP], BF16, st)
                    for c in range(NCD):
                        dma("sp", aT[:, c, :], gT[(hf * NCD + c) * 128:(hf * NCD + c + 1) * 128, p * TP:(p + 1) * TP], r=["gT"], w=["aT"])
                    prep, cons = make_resid(st, gate, p * TP, tag)
                    rp = [(lambda k, c0=c0, n=n: aT[:, k, c0:c0 + n], n) for (c0, n) in pieces(TP)]
                    wdd = wd if nhalf == 1 else wd[hf]
                    proj_ws(wdd, list(range(NCD)), NCD, rp, cons, ["aT"], prepare=prep)
                S.barrier()

    out_proj(hy_wout, 1, MOD(0, 2), "h3")

    if getattr(cfg, "STOP", 99) <= 3:
        S.emit(); es0.close(); return nc, names_in
    def ffn(l):
        sthx = contextlib.ExitStack()
        hx = make_hx(sthx, wm[:, (2 * l + 1) * NCD:(2 * l + 2) * NCD], MOD(l, 3))
        for p in range(2):
            with contextlib.ExitStack() as st:
                hT = make_hT(st, p, wm[:, (2 * l + 1) * NCD:(2 * l + 2) * NCD], MOD(l, 3), f"f{l}_{p}", hx=hx)
                msp, mep = load_masks(st, p)
                cw = sb("ffn_cw_s", [128, 6 * NFF], F32, st)
                dma("sp", cw[:], ffn_cw[l][:, :], w=["cw"])
                ub = [[sb(f"fub{i}_{w_}", [128, TP + 2], F32, st) for w_ in range(2)] for i in range(2)]
                oo = [[sb(f"foo{i}_{w_}", [128, TP], F32, st) for w_ in range(2)] for i in range(2)]
                t1 = sb("fcv_t1", [128, TP], F32, st)
                sl = sb("fsl", [128, TP], F32, st)
                emax = max(cfg.ESZ)
                actT = sb("actT", [128, emax, TP], BF16, st)
                prep, cons = make_resid(st, MOD(l, 5), p * TP, f"f{l}")
                for i in range(2):
                    for w_ in range(2):
                        memset(ub[i][w_][:, :], 0.0, w=[("ub", i, w_)])
                rp = [(lambda k, c0=c0, n=n: hT[:, k, c0:c0 + n], n) for (c0, n) in pieces(TP)] + [(lambda k: hT[:, k, TP:TP + 1], 1)]

                def finish(i, jj):
                    act(sl[:, :], oo[i][0][:, :], AF.Silu, r=[("oo", i, 0)], w=["fsl"])
                    tt(actT[:, jj, :], sl[:, :], oo[i][1][:, :], ALU.mult, r=["fsl", ("oo", i, 1)], w=["actT"])
                j0 = 0
                for e in range(cfg.NE):
                    pend = None
                    for jj in range(cfg.ESZ[e]):
                        j = j0 + jj
                        i = j % 2
                        for w_ in range(2):
                            ev = evac_ub(ub[i][w_], p, ("ub", i, w_))
                            proj_ws(ffn_up[l], [w_ * NFF + j], NCD, rp, lambda bi, pi, b, n, ev=ev: ev(pi, b, n), ["hT"])
                        for w_ in range(2):
                            idx = w_ * NFF + j
                            conv3(oo[i][w_], ub[i][w_], t1, cw[:, idx:idx + 1], cw[:, 2 * NFF + idx:2 * NFF + idx + 1],
                                  cw[:, 4 * NFF + idx:4 * NFF + idx + 1], None, msp[:, :], mep[:, :], [("ub", i, w_)], ("oo", i, w_))
                        if pend is not None:
                            finish(*pend)
                        pend = (i, jj)
                    finish(*pend)
                    rpd = [(lambda k, c0=c0, n=n: actT[:, k, c0:c0 + n], n) for (c0, n) in pieces(TP)]
                    proj_ws(ffn_dn[l][e], list(range(NCD)), cfg.ESZ[e], rpd, cons, ["actT"], prepare=prep)
                    j0 += cfg.ESZ[e]
            S.barrier()
        sthx.close()
        S.barrier()

    ffn(0)

    if getattr(cfg, "STOP", 99) <= 4:
        S.emit(); es0.close(); return nc, names_in
    lg = sb("lg", [128, 2 * NH])
    kdec = sb("kdec", [128, 2 * NH])
    g128 = sb("g128", [128, 2 * NH])
    MT = sb("MT", [128, NH, 128])
    rt = sb("rt", [128, 7 * 128])
    cms = sb("cms", [128, 2 * NG])
    with contextlib.ExitStack() as st:
        kp = sb("kp", [128, 2 * NH], F32, st)
        e1 = sb("e1", [128, 128], F32, st); e2 = sb("e2", [128, 128], F32, st)
        dma("sp", lg[:], ret_lg[:, :], w=["lg"])
        dma("sp", rt[:], rtab[:, :], w=["rt"])
        dma("sp", kp[:], kpos[:, :], w=["kp"])
        dma("sp", cms[:, 0:NG], cmf[:, :], w=["cms"])
        dma("sp", cms[:, NG:2 * NG], cmb[:, :], w=["cms"])
        act(lg[:], lg[:], AF.Exp, scale=-1.0, r=["lg"], w=["lg"])
        act(lg[:], lg[:], AF.Ln, bias=1.0, r=["lg"], w=["lg"])
        tsc1(lg[:], lg[:], -1.0, ALU.mult, r=["lg"], w=["lg"])
        tt(kdec[:], kp[:], lg[:], ALU.mult, r=["kp", "lg"], w=["kdec"])
        act(kdec[:], kdec[:], AF.Exp, r=["kdec"], w=["kdec"])
        tsc1(kdec[:], kdec[:], 0.0625, ALU.mult, r=["kdec"], w=["kdec"])
        act(g128[:], lg[:], AF.Exp, scale=128.0, r=["lg"], w=["g128"])
        for h in range(NH):
            act(e1[:], rt[:, 0:128], AF.Exp, scale=lg[:, h:h + 1], r=["rt", "lg"], w=["e1"])
            act(e2[:], rt[:, 128:256], AF.Exp, scale=lg[:, NH + h:NH + h + 1], r=["rt", "lg"], w=["e2"])
            tt(e1[:], e1[:], rt[:, 256:384], ALU.mult, r=["e1", "rt"], w=["e1"])
            tt(e2[:], e2[:], rt[:, 384:512], ALU.mult, r=["e2", "rt"], w=["e2"])
            tt(e1[:], e1[:], e2[:], ALU.add, r=["e1", "e2"], w=["e1"])
            tt(e1[:], e1[:], rt[:, 512:640], ALU.add, r=["e1", "rt"], w=["e1"])
            tsc1(MT[:, h, :], e1[:], 0.0625, ALU.mult, r=["e1"], w=["MT"])
    S.barrier()

    if getattr(cfg, "STOP", 99) == 45:
        S.emit(); es0.close(); return nc, names_in
    for p in range(2):
        with contextlib.ExitStack() as st:
            hT = sb("hT_r", [128, NCD, TP], BF16, st)
            with contextlib.ExitStack() as st2:
                norm_mod(st2, p * TP, TP, wm[:, 2 * NCD:3 * NCD], MOD(1, 0), lambda c, c0, n: hT[:, c, c0:c0 + n], "hT")
            S.barrier()
            tabs = [sb(f"rope{i}", [128, TP], F32, st) for i in range(2)]
            for i, src in enumerate((cosq, sinq)):
                dma("sp", tabs[i][:], src[:, p * TP:(p + 1) * TP], w=["rope"])
            qk = [sb(f"qkb{i}", [128, 2, TP], BF16, st) for i in range(2)]
            r1 = sb("r1", [128, 512], F32, st); r2 = sb("r2", [128, 512], F32, st)
            kfb = sb("kfb", [128, GP, 256], BF16, st); kbb = sb("kbb", [128, GP, 256], BF16, st)
            wbig = [sb(f"wbig{i}", [128, NCD * 256], BF16, st) for i in range(2)]
            vb = [sb(f"vb{i}", [128, 512], BF16, st) for i in range(2)]
            gnb = sb("gnb", [128, 512], F32, st)
            sgt = sb("sgt", [128, 512], F32, st)
            pcs = pieces(TP)
            rp = [(lambda k, c0=c0, n=n: hT[:, k, c0:c0 + n], n) for (c0, n) in pcs]
            vctr = 0
            vctr2 = [0]
            for h in range(NH):
                for qi, (ct, sn, dd) in enumerate(((tabs[0], tabs[1], qT_d), (tabs[0], tabs[1], kT_d))):
                    banks = {}

                    def cons(bi, pi, b, n, banks=banks, qi=qi, ct=ct, sn=sn):
                        banks[(bi, pi)] = b
                        if bi == 1:
                            c0 = pcs[pi][0]
                            ba, bb = banks[(0, pi)], b
                            tt(r1[:, 0:n], ps[ba][:, 0:n], ct[:, c0:c0 + n], ALU.mult, r=[("ps", ba), "rope"], w=["r1"])
                            tt(r2[:, 0:n], ps[bb][:, 0:n], sn[:, c0:c0 + n], ALU.mult, r=[("ps", bb), "rope"], w=["r2"])
                            tt(qk[qi][:, 0, c0:c0 + n], r1[:, 0:n], r2[:, 0:n], ALU.subtract, r=["r1", "r2"], w=[("qk", qi)])
                            tt(r1[:, 0:n], ps[ba][:, 0:n], sn[:, c0:c0 + n], ALU.mult, r=[("ps", ba), "rope"], w=["r1"])
                            tt(r2[:, 0:n], ps[bb][:, 0:n], ct[:, c0:c0 + n], ALU.mult, r=[("ps", bb), "rope"], w=["r2"])
                            tt(qk[qi][:, 1, c0:c0 + n], r1[:, 0:n], r2[:, 0:n], ALU.add, r=["r1", "r2"], w=[("qk", qi)])
                    proj_ws(ret_qk, [h * 4 + qi * 2, h * 4 + qi * 2 + 1], NCD, rp, cons, ["hT"])
                    for dc in range(2):
                        dma("sp", dd[(h * 2 + dc) * 128:(h * 2 + dc + 1) * 128, p * TP:(p + 1) * TP], qk[qi][:, dc, :], r=[("qk", qi)], w=["qkT_d"])
                for half in range(0 if getattr(cfg, "SKIP_KT", 0) else 2 * GP // 8 if 2 * GP >= 8 else 1):
                    tb = ntbank()
                    items = [(g, dc) for g in range(GP) for dc in range(2)][half * 8:(half + 1) * 8]
                    for ii, (g, dc) in enumerate(items):
                        tr(pst[tb][:, ii * 128:(ii + 1) * 128], qk[1][:, dc, g * 128:(g + 1) * 128], idb[:], r=[("qk", 1), "idb"], w=[("pst", tb)])
                    g0 = items[0][0]
                    ng_ = len(items) // 2
                    src_ap = lambda tb=tb, ng_=ng_: pst[tb][:, 0:ng_ * 256].rearrange("p (g c) -> p g c", g=ng_)
                    S.add("act", lambda e, src_ap=src_ap, g0=g0, ng_=ng_, h=h: e.activation(out=kfb[:, g0:g0 + ng_, :], in_=src_ap(), func=AF.Identity, scale=kdec[:, h:h + 1]),
                          [("pst", tb), "kdec"], ["kfb"])
                    S.add("act", lambda e, src_ap=src_ap, g0=g0, ng_=ng_, h=h: e.activation(out=kbb[:, g0:g0 + ng_, :], in_=src_ap(), func=AF.Identity, scale=kdec[:, NH + h:NH + h + 1]),
                          [("pst", tb), "kdec"], ["kbb"])
                dma("sp", kf_d[p * TP:(p + 1) * TP, h * 256:(h + 1) * 256].rearrange("(g q) c -> q g c", q=128), kfb[:, :, :], r=["kfb"], w=["kf_d"])
                dma("sp", kb_d[p * TP:(p + 1) * TP, h * 256:(h + 1) * 256].rearrange("(g q) c -> q g c", q=128), kbb[:, :, :], r=["kbb"], w=["kb_d"])
                dma("sp", gnb[:], ret_gn[:, h * 512:(h + 1) * 512], w=["gnb"])
                for which in range(0 if getattr(cfg, "SKIP_VG", 0) else 2):
                  for hv in range(2):
                    wi = vctr % 2
                    dma("pq", wbig[wi][:], ret_vg[(h * 2 + which) * 2 + hv], w=[("wbig", wi)])
                    for g in range(GP):
                        b = nbank()
                        for k in range(NCD):
                            mm(ps[b][:, 0:256], hT[:, k, g * 128:(g + 1) * 128], wbig[wi][:, k * 256:(k + 1) * 256], k == 0, k == NCD - 1,
                               r=["hT", ("wbig", wi)], w=[("ps", b)])
                        vi = (vctr2[0]) % 2
                        vctr2[0] += 1
                        rows = slice(p * TP + g * 128, p * TP + (g + 1) * 128)
                        cols = slice(h * 512 + hv * 256, h * 512 + (hv + 1) * 256)
                        if which == 0:
                            act(vb[vi][:, 0:256], ps[b][:, 0:256], AF.Copy, r=[("ps", b)], w=[("vb", vi)])
                            dma("sp", v_d[rows, cols], vb[vi][:, 0:256], r=[("vb", vi)], w=["v_d"])
                        else:
                            act(sgt[:, 0:256], ps[b][:, 0:256], AF.Silu, r=[("ps", b)], w=["sgt"])
                            tt(vb[vi][:, 0:256], sgt[:, 0:256], gnb[:, hv * 256:(hv + 1) * 256], ALU.mult, r=["sgt", "gnb"], w=[("vb", vi)])
                            dma("sp", sg_d[rows, cols], vb[vi][:, 0:256], r=[("vb", vi)], w=["sg_d"])
                    vctr += 1
        S.barrier()

    if getattr(cfg, "STOP", 99) <= 5:
        S.emit(); es0.close(); return nc, names_in
    with contextlib.ExitStack() as st:
        qTh = sb("qTh", [128, 2, TS], BF16, st); kTh = sb("kTh", [128, 2, TS], BF16, st)
        qfc = [sb(f"qfc{i}", [128, 2, 128], BF16, st) for i in range(2)]
        qbc = [sb(f"qbc{i}", [128, 2, 128], BF16, st) for i in range(2)]
        kfh = sb("kfh", [128, NG, 256], BF16, st); kbh = sb("kbh", [128, NG, 256], BF16, st)
        vh = sb("vh", [128, NG, 512], BF16, st); sgh = sb("sgh", [128, NG, 512], BF16, st)
        R = sb("Rst", [128, 2, 512], F32, st)
        Efc = [sb(f"Efc{i}", [128, 1024], BF16, st) for i in range(2)]
        Eb = sb("Eb", [128, NG, 1024], BF16, st)
        stg = [sb(f"stg{i}", [128, 2, 512], F32, st) for i in range(2)]
        cg = sb("cg", [128, 2 * NG], F32, st)
        qd = sb("qd", [128, 2, 256], F32, st)
        PT = [sb(f"PT{i}", [128, 128], BF16, st) for i in range(2)]
        ssum = sb("ssum", [128, 2], F32, st); junk = sb("junk", [128, 512], F32, st)
        gob = sb("gob", [128, 512], BF16, st)
        goT = sb("goT", [128, 4, TS], BF16, st)
        sctr = 0
        Rflat = R[:, :, :].rearrange("p a v -> p (a v)")
        for h in range(NH):
            for dc in range(2):
                dma("sp", qTh[:, dc, :], qT_d[(h * 2 + dc) * 128:(h * 2 + dc + 1) * 128, :], r=["qkT_d"], w=["qTh"])
                dma("sp", kTh[:, dc, :], kT_d[(h * 2 + dc) * 128:(h * 2 + dc + 1) * 128, :], r=["qkT_d"], w=["kTh"])
            dma("sp", kfh[:, :, :], kf_d[:, h * 256:(h + 1) * 256].rearrange("(g q) c -> q g c", q=128), r=["kf_d"], w=["kfh"])
            dma("sp", kbh[:, :, :], kb_d[:, h * 256:(h + 1) * 256].rearrange("(g q) c -> q g c", q=128), r=["kb_d"], w=["kbh"])
            dma("sp", vh[:, :, :], v_d[:, h * 512:(h + 1) * 512].rearrange("(g q) c -> q g c", q=128), r=["v_d"], w=["vh"])
            dma("sp", sgh[:, :, :], sg_d[:, h * 512:(h + 1) * 512].rearrange("(g q) c -> q g c", q=128), r=["sg_d"], w=["sgh"])
            tsc1(cg[:, 0:NG], cms[:, 0:NG], g128[:, h:h + 1], ALU.mult, r=["cms", "g128"], w=["cg"])
            tsc1(cg[:, NG:2 * NG], cms[:, NG:2 * NG], g128[:, NH + h:NH + h + 1], ALU.mult, r=["cms", "g128"], w=["cg"])
            for dc in range(2):
                act(qd[:, 0, dc * 128:(dc + 1) * 128], rt[:, 640:768], AF.Exp, scale=lg[:, h:h + 1], r=["rt", "lg"], w=["qd"])
                act(qd[:, 1, dc * 128:(dc + 1) * 128], rt[:, 768:896], AF.Exp, scale=lg[:, NH + h:NH + h + 1], r=["rt", "lg"], w=["qd"])

            def kv_update(kk, d_, c):
                nonlocal sctr
                cgo = d_ * NG
                for dc in range(2):
                    b = nbank()
                    mm(ps[b][:, :], kk[:, c, dc * 128:(dc + 1) * 128], vh[:, c, :], True, True, r=["vh", "kfh", "kbh"], w=[("ps", b)])
                    stt(R[:, dc, :], R[:, dc, :], cg[:, cgo + c:cgo + c + 1], ps[b][:, :], ALU.mult, ALU.add, r=["R", "cg", ("ps", b)], w=["R"])
                emit = ((c + 1) % cfg.CPS == 0) if d_ == 0 else (c % cfg.CPS == 0)
                if emit:
                    si = sctr % 2
                    sctr += 1
                    act(stg[si][:, :, :], R[:, :, :], AF.Copy, r=["R"], w=[("stg", si)])
                    dma("sp", st_out[d_, h, c // cfg.CPS].rearrange("(a q) v -> q a v", q=128), stg[si][:, :, :], r=[("stg", si)], w=["st_out"])

            dma("sp", R[:, :, :], s0[1, h].rearrange("(a q) v -> q a v", q=128), w=["R"])
            for c in range(NG - 1, -1, -1):
                S.add("pool", lambda e, c=c: e.tensor_scalar(out=Eb[:, c, :], in0=Rflat, scalar1=cms[:, NG + c:NG + c + 1], scalar2=None, op0=ALU.mult), ["R", "cms"], ["Eb"])
                kv_update(kbh, 1, c)
            dma("sp", R[:, :, :], s0[0, h].rearrange("(a q) v -> q a v", q=128), w=["R"])
            for c in range(NG):
                csl = slice(c * 128, (c + 1) * 128)
                pi = c % 2
                S.add("pool", lambda e, c=c, pi=pi: e.tensor_scalar(out=Efc[pi][:, :], in0=Rflat, scalar1=cms[:, c:c + 1], scalar2=None, op0=ALU.mult), ["R", "cms"], [("Efc", pi)])
                tt(qfc[pi][:, :, :], qTh[:, :, csl], qd[:, 0, :].rearrange("p (a b) -> p a b", a=2), ALU.mult, r=["qTh", "qd"], w=[("qfc", pi)])
                tt(qbc[pi][:, :, :], qTh[:, :, csl], qd[:, 1, :].rearrange("p (a b) -> p a b", a=2), ALU.mult, r=["qTh", "qd"], w=[("qbc", pi)])
                b = nbank()
                for dc in range(2):
                    mm(ps[b][:, 0:128], kTh[:, dc, csl], qTh[:, dc, csl], dc == 0, dc == 1, r=["kTh", "qTh"], w=[("ps", b)])
                tt(PT[pi][:], ps[b][:, 0:128], MT[:, h, :], ALU.mult, r=[("ps", b), "MT"], w=[("PT", pi)])
                bo = nbank()
                mm(ps[bo][:, :], PT[pi][:], vh[:, c, :], True, False, r=[("PT", pi), "vh"], w=[("ps", bo)])
                for dc in range(2):
                    mm(ps[bo][:, :], qfc[pi][:, dc, :], Efc[pi][:, dc * 512:(dc + 1) * 512], False, False, r=[("qfc", pi), ("Efc", pi)], w=[("ps", bo)])
                for dc in range(2):
                    mm(ps[bo][:, :], qbc[pi][:, dc, :], Eb[:, c, dc * 512:(dc + 1) * 512], False, dc == 1, r=[("qbc", pi), "Eb"], w=[("ps", bo)])
                act(junk[:], ps[bo][:, :], AF.Square, accum_out=ssum[:, 0:1], r=[("ps", bo)], w=["ssum", "junk"])
                tsc(ssum[:, 1:2], ssum[:, 0:1], 1.0 / 512, EPS, ALU.mult, ALU.add, r=["ssum"], w=["ssum2"])
                rsqrt_(ssum[:, 1:2], "ssum2")
                stt(gob[:], ps[bo][:, :], ssum[:, 1:2], sgh[:, c, :], ALU.mult, ALU.mult, r=[("ps", bo), "ssum2", "sgh"], w=["gob"])
                tb = ntbank()
                for cc in range(4):
                    tr(pst[tb][:, cc * 128:(cc + 1) * 128], gob[:, cc * 128:(cc + 1) * 128], idb[:], r=["gob", "idb"], w=[("pst", tb)])
                S.add("act", lambda e, tb=tb, c=c: e.activation(out=goT[:, :, c * 128:(c + 1) * 128], in_=pst[tb][:, 0:512].rearrange("p (c t) -> p c t", c=4), func=AF.Copy),
                      [("pst", tb)], ["goT"])
                kv_update(kfh, 0, c)
            for cc in range(4):
                dma("sp", gT[(h * 4 + cc) * 128:(h * 4 + cc + 1) * 128, :], goT[:, cc, :], r=["goT"], w=["gT"])
    S.barrier()

    if getattr(cfg, "STOP", 99) <= 6:
        S.emit(); es0.close(); return nc, names_in
    out_proj(ret_wo, 2, MOD(1, 2), "r3")
    ffn(1)

    for p in range(2):
        with contextlib.ExitStack() as st:
            norm_mod(st, p * TP, TP, wm[:, 4 * NCD:5 * NCD], zshift, None, "yfin", out_dram=yT)
        S.barrier()

    S.emit()
    es0.close()
    return nc, names_in

BF = ml_dtypes.bfloat16


def ws_blocks(W):
    K, C = W.shape
    return np.ascontiguousarray(W.reshape(K // 128, 128, C // 128, 128).transpose(2, 1, 0, 3)).reshape(C // 128, 128, (K // 128) * 128)


def as_blocks(W, n):
    K, C = W.shape
    return np.ascontiguousarray(W.reshape(K // 128, 128, C // n, n).transpose(2, 1, 0, 3)).reshape(C // n, 128, (K // 128) * n)


def colT(v, n=None):
    v = np.asarray(v, np.float32)
    return np.ascontiguousarray(v.reshape(-1, 128).T)


def rep(v):
    return np.ascontiguousarray(np.broadcast_to(np.asarray(v, np.float32)[None, :], (128, len(v))))


def dft_tables(cfg, is_sample):
    TS, NG = cfg.TS, cfg.NG
    L = TS if is_sample else cfg.LP
    n = 2 * L
    nblk = TS // L
    j = np.arange(L)[:, None].astype(np.float64)
    f = np.arange(L)[None, :].astype(np.float64)
    ang = 2 * np.pi * j * f / n
    fre = np.cos(ang)
    fim = -np.sin(ang); fim[:, 0] = 0.0
    fn = np.zeros((L, L)); fn[:, 0] = np.cos(np.pi * np.arange(L))
    t = np.arange(L)[None, :].astype(np.float64)
    ff = np.arange(L)[:, None].astype(np.float64)
    ang2 = 2 * np.pi * ff * t / n
    gre = (2.0 / n) * np.cos(ang2); gre[0, :] = 1.0 / n
    gim = -(2.0 / n) * np.sin(ang2); gim[0, :] = (1.0 / n) * np.cos(np.pi * np.arange(L))

    def bd(m):
        out = np.zeros((TS, TS))
        for b in range(nblk):
            out[b * L:(b + 1) * L, b * L:(b + 1) * L] = m
        return out

    def fl(m):
        return np.ascontiguousarray(m.reshape(NG, 128, NG, 128).transpose(2, 1, 0, 3)).reshape(NG, 128, NG * 128).astype(BF)
    Fre, Fim, Fn_ = bd(fre), bd(fim), bd(fn)
    G = np.concatenate([bd(gre), bd(gim)], axis=0)
    Gm = np.ascontiguousarray(G.reshape(2 * NG, 128, NG, 128).transpose(2, 1, 0, 3)).reshape(NG, 128, 2 * NG * 128).astype(BF)
    sp = np.zeros(TS); sp[::L] = 1.0
    return dict(Fre=fl(Fre), Fim=fl(Fim), Fzi=fl(Fim + Fn_), Fn=fl(Fn_), Gm=Gm,
                spT=colT(sp), nsT=colT(1.0 - sp))


def host_inputs(cfg, inp):
    D, NCD, NH, TS, NG, NFF, LP = cfg.D, cfg.NCD, cfg.NH, cfg.TS, cfg.NG, cfg.NFF, cfg.LP
    f32 = lambda a: np.asarray(a, np.float32)
    shared = {}
    ada_w = f32(inp["ada_w"])
    shared["ada_wb"] = np.concatenate([ws_blocks(ada_w[l]) for l in range(2)], axis=0)
    shared["ada_bT"] = colT(f32(inp["ada_b"]).reshape(-1))
    shared["n1T"] = colT(f32(inp["norm1_w"]).reshape(-1))
    shared["n2T"] = colT(f32(inp["norm2_w"]).reshape(-1))
    shared["nfT"] = colT(f32(inp["final_norm_w"]))
    wb = ws_blocks(f32(inp["hy_w_in"])[0])
    shared["hy_win"] = np.ascontiguousarray(wb.reshape(3, NCD, 128, -1).transpose(1, 0, 2, 3)).reshape(3 * NCD, 128, -1)
    sw = f32(inp["hy_short_w"])[0]
    shared["hy_cw"] = np.concatenate([colT(sw[t]) for t in range(3)], axis=1)
    shared["hy_sb"] = colT(f32(inp["hy_short_b"])[0])
    shared["hy_w1"] = f32(inp["hy_f_w1"])[0]; shared["hy_w2"] = f32(inp["hy_f_w2"])[0]; shared["hy_w3"] = f32(inp["hy_f_w3"])[0]
    fr = f32(inp["hy_f_freq"])[0]
    shared["hy_bf"] = np.ascontiguousarray(np.stack([f32(inp["hy_f_b1"])[0], f32(inp["hy_f_b2"])[0], f32(inp["hy_f_b3"])[0], fr[0], fr[1], fr[2]], axis=1))
    shared["hy_wo"] = f32(inp["hy_f_wout"])[0]
    shared["hy_fb"] = rep(f32(inp["hy_f_bias"])[0])
    shared["hy_wout"] = ws_blocks(f32(inp["hy_w_out"])[0])
    deltas = np.abs(np.linspace(math.log(0.3) / 1e-2, math.log(1.5) / 1e-2, D, dtype=np.float32))
    shared["deltab"] = rep(deltas)
    for l in range(2):
        shared[f"ffn_up{l}"] = ws_blocks(f32(inp["ffn_w_up"])[l])
        cw = f32(inp["ffn_conv_w"])[l]
        shared[f"ffn_cw{l}"] = np.concatenate([colT(cw[t]) for t in range(3)], axis=1)
        wd = f32(inp["ffn_w_down"])[l]
        r0 = 0
        for e in range(cfg.NE):
            n = cfg.ESZ[e] * 128
            shared[f"ffn_dn{l}_{e}"] = ws_blocks(wd[r0:r0 + n])
            r0 += n
    rw = f32(inp["ret_w_in"])[0]
    qb_ = ws_blocks(rw[:, 0:D]); kb_ = ws_blocks(rw[:, D:2 * D])
    shared["ret_qk"] = np.ascontiguousarray(np.stack([qb_.reshape(NH, 2, 128, -1), kb_.reshape(NH, 2, 128, -1)], axis=1)).reshape(4 * NH, 128, -1)
    vb_ = as_blocks(rw[:, 2 * D:4 * D], 256).reshape(NH, 2, 128, -1); gb_ = as_blocks(rw[:, 4 * D:6 * D], 256).reshape(NH, 2, 128, -1)
    shared["ret_vg"] = np.ascontiguousarray(np.stack([vb_, gb_], axis=1)).reshape(4 * NH, 128, -1)
    wo = f32(inp["ret_w_out"])[0]
    shared["ret_wo"] = np.stack([ws_blocks(wo[0:D]), ws_blocks(wo[D:2 * D])], axis=0)
    shared["ret_lg"] = rep(f32(inp["ret_decay_logit"])[0].reshape(-1))
    shared["ret_gn"] = rep(f32(inp["ret_gn_w"])[0])
    i_ = np.arange(128)[None, :]; j_ = np.arange(128)[:, None]
    rt = [np.maximum(i_ - j_, 0), np.maximum(j_ - i_, 0), (i_ > j_), (j_ > i_), 2.0 * np.eye(128),
          np.broadcast_to(i_ + 1, (128, 128)), np.broadcast_to(128 - i_, (128, 128))]
    shared["rtab"] = np.ascontiguousarray(np.concatenate([np.asarray(a, np.float32) for a in rt], axis=1))
    p_ = np.arange(128, dtype=np.float32)[:, None]
    shared["kpos"] = np.ascontiguousarray(np.concatenate([np.broadcast_to(127 - p_, (128, NH)), np.broadcast_to(p_, (128, NH))], axis=1))
    shared["identb"] = np.eye(128, dtype=np.float32).astype(BF)

    def span_tables(is_sample):
        d = dft_tables(cfg, is_sample)
        L = TS if is_sample else LP
        pos = np.arange(TS) % L
        d["mS"] = rep((pos != 0).astype(np.float32))
        d["mE"] = rep((pos != L - 1).astype(np.float32))
        t = np.linspace(0.0, 1.0, L, dtype=np.float32)
        w = 2.0 * np.pi * np.arange(L, dtype=np.float32) / L
        f = np.linspace(1e-4, 15.0, 16, dtype=np.float32)[None, :]
        z = np.concatenate([t[:, None], np.cos(f * w[:, None]), -np.sin(f * w[:, None])], axis=-1).astype(np.float32)
        d["zembT"] = np.ascontiguousarray(np.tile(z, (TS // L, 1)).T)
        d["negt"] = colT(-np.tile(t, TS // L))
        d["nfirst"] = colT((pos != 0).astype(np.float32))
        if is_sample:
            GW = 64
            rows = TS // GW
            row = np.repeat(np.arange(rows, dtype=np.float32), GW); col = np.tile(np.arange(GW, dtype=np.float32), rows)
            inv = (10000.0 ** (-np.arange(64, dtype=np.float32) / 64)).astype(np.float32)
            ang = np.concatenate([row[:, None] * inv, col[:, None] * inv], axis=-1)
            d["cosq"] = np.ascontiguousarray(np.cos(ang).T.astype(np.float32)); d["sinq"] = np.ascontiguousarray(np.sin(ang).T.astype(np.float32))
            d["cmf"] = rep(np.ones(NG)); d["cmb"] = rep(np.ones(NG))
        else:
            d["cosq"] = np.ones((128, TS), np.float32); d["sinq"] = np.zeros((128, TS), np.float32)
            c = np.arange(NG)
            cmf = (c % cfg.CPS != 0).astype(np.float32); cmf[0] = 1.0
            cmb = (c % cfg.CPS != cfg.CPS - 1).astype(np.float32); cmb[NG - 1] = 1.0
            d["cmf"] = rep(cmf); d["cmb"] = rep(cmb)
        return d
    tab_s, tab_p = span_tables(True), span_tables(False)
    maps = []
    xs, xp = f32(inp["x_sample"]), f32(inp["x_prompt"])
    for core in range(cfg.NCORES):
        m = dict(shared)
        if core < 2:
            m.update(tab_s)
            m["xT0"] = np.ascontiguousarray(xs[core].T)
            m["cvec"] = colT(f32(inp["c"])[core])
            m["s0"] = np.ascontiguousarray(f32(inp["state_retention"])[core, 0])
        else:
            m.update(tab_p)
            b0 = (core - 2) * cfg.NPC
            m["xT0"] = np.ascontiguousarray(xp[b0:b0 + cfg.NPC].reshape(TS, D).T)
            m["cvec"] = colT(f32(inp["c_ctx"]))
            m["s0"] = np.zeros((2, NH, 256, 512), np.float32)
        maps.append(m)
    return maps


_CACHE = {}


def run(cfg, inp, trace=False):
    key = id(cfg)
    if key not in _CACHE:
        _CACHE[key] = build(cfg)
    nc, names = _CACHE[key]
    maps = host_inputs(cfg, inp)
    maps = [{k: m[k] for k in names} for m in maps]
    res = run_bass_kernel_spmd(nc, maps, core_ids=list(range(cfg.NCORES)), trace=trace)
    D, TS, NH = cfg.D, cfg.TS, cfg.NH
    outs = res.results
    y_sample = np.stack([np.ascontiguousarray(outs[c]["yT"].T) for c in range(2)], axis=0)
    y_prompt = np.concatenate([np.ascontiguousarray(outs[c]["yT"].T).reshape(cfg.NPC, cfg.LP, D) for c in range(2, cfg.NCORES)], axis=0)
    st = np.concatenate([outs[c]["st_out"].transpose(2, 0, 1, 3, 4) for c in range(2, cfg.NCORES)], axis=0)[:, None]
    return (y_prompt.astype(np.float32), y_sample.astype(np.float32), np.ascontiguousarray(st).astype(np.float32)), res


def kernel(**inputs):
    out, _ = run(FULL, inputs)
    return out
```

```python
import concourse.bass as bass
import concourse.mybir as mybir

SEM_CAP = 24000
DMA_SLOTS = 6
COMPUTE = ("pe", "act", "dve", "pool")
QUEUES = ("sp", "pq")


class Op:
    __slots__ = ("eng", "fn", "deps", "raw", "sig", "idx", "dma_slot", "dma_val", "is_dma", "nsig")

    def __init__(self, eng, fn, is_dma):
        self.eng = eng
        self.fn = fn
        self.deps = set()
        self.raw = set()
        self.sig = False
        self.is_dma = is_dma
        self.dma_slot = None
        self.dma_val = None
        self.nsig = None


class Sched:
    def __init__(self, nc):
        self.nc = nc
        self.ops = []
        self.last_w = {}
        self.readers = {}
        self.bar_deps = set()
        self.last_on = {}

    def add(self, eng, fn, reads=(), writes=()):
        is_dma = eng in QUEUES
        op = Op(eng, fn, is_dma)
        i = len(self.ops)
        op.idx = i
        for k in reads:
            w = self.last_w.get(k)
            if w is not None:
                op.deps.add(w)
                op.raw.add(w)
            self.readers.setdefault(k, []).append(i)
        for k in writes:
            w = self.last_w.get(k)
            if w is not None:
                op.deps.add(w)
            for r in self.readers.get(k, ()):
                if r != i:
                    op.deps.add(r)
            self.last_w[k] = i
            self.readers[k] = []
        op.deps |= self.bar_deps
        if is_dma:
            self.last_on.setdefault(eng, []).append(i)
            if len(self.last_on[eng]) > DMA_SLOTS:
                self.last_on[eng] = self.last_on[eng][-DMA_SLOTS:]
        else:
            self.last_on[eng] = i
        self.ops.append(op)
        return i

    def barrier(self):
        deps = set()
        for e, v in self.last_on.items():
            if isinstance(v, list):
                deps.update(v)
            else:
                deps.add(v)
        self.bar_deps = deps

    def emit(self):
        nc = self.nc
        ops = self.ops
        stream_of = {"pe": "pe", "act": "act", "dve": "dve", "pool": "pool", "sp": "sp", "pq": "pool"}
        for op in ops:
            keep = set()
            best = {}
            for d in op.deps:
                p = ops[d]
                if p.is_dma:
                    keep.add(d)
                    continue
                if p.eng == op.eng and not op.is_dma:
                    if p.eng == "pe":
                        continue
                    if d not in op.raw:
                        continue
                if best.get(p.eng, -1) < d:
                    best[p.eng] = d
            keep.update(best.values())
            op.deps = keep
        qcount = {q: 0 for q in QUEUES}
        qhist = {q: [] for q in QUEUES}
        for op in ops:
            if op.is_dma:
                n = qcount[op.eng]
                op.dma_slot = n % DMA_SLOTS
                op.dma_val = 16 * (n // DMA_SLOTS + 1)
                if n >= DMA_SLOTS:
                    op.deps.add(qhist[op.eng][n - DMA_SLOTS])
                qhist[op.eng].append(op.idx)
                qcount[op.eng] = n + 1
        for op in ops:
            for d in op.deps:
                ops[d].sig = True
        cnt = {e: 0 for e in COMPUTE}
        for op in ops:
            if not op.is_dma and op.sig:
                op.nsig = cnt[op.eng]
                cnt[op.eng] += 1
        import contextlib
        es = contextlib.ExitStack()
        sems = {}
        for e in COMPUTE:
            n = (cnt[e] + SEM_CAP - 1) // SEM_CAP
            sems[e] = [es.enter_context(nc.semaphore(f"s_{e}{j}")) for j in range(max(n, 1))]
        dsem = {q: [es.enter_context(nc.semaphore(f"d_{q}{j}")) for j in range(DMA_SLOTS)] for q in QUEUES}
        self.nsems = sum(len(v) for v in sems.values()) + 2 * DMA_SLOTS
        streams = {"pe": [], "act": [], "dve": [], "pool": [], "sp": []}
        for op in ops:
            streams[stream_of[op.eng]].append(op)

        def run_stream(engobj, lst):
            waited = {}
            for op in lst:
                for d in sorted(op.deps):
                    p = ops[d]
                    if p.is_dma:
                        s = dsem[p.eng][p.dma_slot]
                        v = p.dma_val
                    else:
                        s = sems[p.eng][p.nsig // SEM_CAP]
                        v = p.nsig % SEM_CAP + 1
                    key = id(s)
                    if waited.get(key, 0) >= v:
                        continue
                    waited[key] = v
                    engobj.wait_ge(s, v)
                ins = op.fn(engobj)
                if op.is_dma:
                    ins.then_inc(dsem[op.eng][op.dma_slot], 16)
                elif op.sig:
                    ins.then_inc(sems[op.eng][op.nsig // SEM_CAP], 1)
            return

        with nc.Block() as block:
            @block.tensor
            def _(e):
                run_stream(e, streams["pe"])

            @block.scalar
            def _(e):
                run_stream(e, streams["act"])

            @block.vector
            def _(e):
                run_stream(e, streams["dve"])

            @block.gpsimd
            def _(e):
                run_stream(e, streams["pool"])

            @block.sync
            def _(e):
                run_stream(e, streams["sp"])
                for q in QUEUES:
                    n = qcount[q]
                    for sl in range(DMA_SLOTS):
                        k = (n - sl + DMA_SLOTS - 1) // DMA_SLOTS
                        if k > 0:
                            e.wait_ge(dsem[q][sl], 16 * k)
        es.close()

import contextlib
import math
import numpy as np
import ml_dtypes
from concourse.bass_utils import run_bass_kernel_spmd

F32 = mybir.dt.float32
BF16 = mybir.dt.bfloat16
AF = mybir.ActivationFunctionType
ALU = mybir.AluOpType
EPS = 1e-6


class Cfg:
    def __init__(self, D, NH, DFF, TS, LP, BATCH, NE):
        self.D, self.NH, self.DFF, self.TS, self.LP, self.BATCH, self.NE = D, NH, DFF, TS, LP, BATCH, NE
        self.NCD = D // 128
        self.NFF = DFF // 128
        self.TP = TS // 2
        self.NG = TS // 128
        self.GP = self.TP // 128
        self.DK, self.DV = 256, 512
        assert D == NH * 256
        self.NPC = TS // LP
        self.NCORES = 2 + BATCH // self.NPC
        self.CPS = LP // 128
        self.NFIN = self.NG // self.CPS
        self.NCB = D // 512
        base, rem = divmod(self.NFF, NE)
        self.ESZ = [base + (1 if i < rem else 0) for i in range(NE)]


FULL = Cfg(D=4096, NH=16, DFF=11008, TS=2048, LP=256, BATCH=16, NE=8)


def pieces(n, mx=512):
    out, c = [], 0
    while c < n:
        m = min(mx, n - c)
        out.append((c, m))
        c += m
    return out


def build(cfg):
    nc = bass.Bass("TRN2", target_bir_lowering=False)
    S = Sched(nc)
    D, NCD, NH, TS, TP, NG, GP, NFF, NCB = cfg.D, cfg.NCD, cfg.NH, cfg.TS, cfg.TP, cfg.NG, cfg.GP, cfg.NFF, cfg.NCB
    NCC = 2 * NG
    names_in = []

    def din(name, shape, dt=F32):
        names_in.append(name)
        return nc.dram_tensor(name, list(shape), dt, kind="ExternalInput").ap()

    def dout(name, shape, dt=F32):
        return nc.dram_tensor(name, list(shape), dt, kind="ExternalOutput").ap()

    def dscr(name, shape, dt=F32):
        return nc.dram_tensor(name, list(shape), dt, kind="Internal").ap()

    xT0 = din("xT0", [D, TS])
    cvec = din("cvec", [128, NCD])
    ada_wb = din("ada_wb", [12 * NCD, 128, NCD * 128])
    ada_bT = din("ada_bT", [128, 12 * NCD])
    n1T = din("n1T", [128, 2 * NCD])
    n2T = din("n2T", [128, 2 * NCD])
    nfT = din("nfT", [128, NCD])
    mS = din("mS", [128, TS])
    mE = din("mE", [128, TS])
    hy_win = din("hy_win", [3 * NCD, 128, NCD * 128])
    hy_cw = din("hy_cw", [128, 3 * 3 * NCD])
    hy_sb = din("hy_sb", [128, 3 * NCD])
    hy_w1 = din("hy_w1", [33, 64]); hy_w2 = din("hy_w2", [64, 64]); hy_w3 = din("hy_w3", [64, 64])
    hy_bf = din("hy_bf", [64, 6])
    hy_wo = din("hy_wo", [64, 2 * D])
    hy_fb = din("hy_fb", [128, D])
    hy_wout = din("hy_wout", [NCD, 128, NCD * 128])
    zembT = din("zembT", [33, TS])
    negt = din("negt", [128, NG]); nfirst = din("nfirst", [128, NG])
    deltab = din("deltab", [128, D])
    nsT = din("nsT", [128, NG]); spT = din("spT", [128, NG])
    Fre = din("Fre", [NG, 128, NG * 128], BF16)
    Fim = din("Fim", [NG, 128, NG * 128], BF16)
    Fzi = din("Fzi", [NG, 128, NG * 128], BF16)
    Fn = din("Fn", [NG, 128, NG * 128], BF16)
    Gm = din("Gm", [NG, 128, NCC * 128], BF16)
    ffn_up = [din(f"ffn_up{l}", [2 * NFF, 128, NCD * 128]) for l in range(2)]
    ffn_cw = [din(f"ffn_cw{l}", [128, 3 * 2 * NFF]) for l in range(2)]
    ffn_dn = [[din(f"ffn_dn{l}_{e}", [NCD, 128, cfg.ESZ[e] * 128]) for e in range(cfg.NE)] for l in range(2)]
    ret_qk = din("ret_qk", [4 * NH, 128, NCD * 128])
    ret_vg = din("ret_vg", [4 * NH, 128, NCD * 256])
    ret_wo = din("ret_wo", [2, NCD, 128, NCD * 128])
    ret_lg = din("ret_lg", [128, 2 * NH])
    ret_gn = din("ret_gn", [128, 2 * D])
    s0 = din("s0", [2, NH, 256, 512])
    cosq = din("cosq", [128, TS]); sinq = din("sinq", [128, TS])
    rtab = din("rtab", [128, 7 * 128])
    kpos = din("kpos", [128, 2 * NH])
    cmf = din("cmf", [128, NG]); cmb = din("cmb", [128, NG])
    identb = din("identb", [128, 128], BF16)

    yT = dout("yT", [D, TS])
    st_out = dout("st_out", [2, NH, cfg.NFIN, 256, 512])

    xT = dscr("xT", [D, TS])
    ztok = dscr("ztok", [TS, D], BF16)
    x0tok = dscr("x0tok", [TS, D], BF16)
    gT = dscr("gT", [2 * D, TS], BF16)
    qT_d = dscr("qT_d", [2 * NH * 128, TS], BF16)
    kT_d = dscr("kT_d", [2 * NH * 128, TS], BF16)
    kf_d = dscr("kf_d", [TS, NH * 256], BF16)
    kb_d = dscr("kb_d", [TS, NH * 256], BF16)
    v_d = dscr("v_d", [TS, NH * 512], BF16)
    sg_d = dscr("sg_d", [TS, NH * 512], BF16)

    es0 = contextlib.ExitStack()
    uniq = [0]

    def sb(name, shape, dt=F32, st=es0):
        uniq[0] += 1
        return st.enter_context(nc.sbuf_tensor(f"{name}_{uniq[0]}", list(shape), dt))
    ps = [es0.enter_context(nc.psum_tensor(f"ps{i}", [128, 512], F32)) for i in range(6)]
    pst = [es0.enter_context(nc.psum_tensor(f"pst{i}", [128, 1024], BF16)) for i in range(2)]
    bank_ctr = [0, 0]

    def nbank():
        b = bank_ctr[0] % 6
        bank_ctr[0] += 1
        return b

    def ntbank():
        b = bank_ctr[1] % 2
        bank_ctr[1] += 1
        return b

    def dma(q, out, in_, r=(), w=()):
        S.add(q, lambda e: e.dma_start(out=out, in_=in_, allow_slow_non_contiguous=True), r, w)

    def act(out, in_, func, r=(), w=(), **kw):
        S.add("act", lambda e: e.activation(out=out, in_=in_, func=func, **kw), r, w)

    def tt(out, in0, in1, op, r=(), w=()):
        S.add("dve", lambda e: e.tensor_tensor(out=out, in0=in0, in1=in1, op=op), r, w)

    def stt(out, in0, scalar, in1, op0, op1, r=(), w=()):
        S.add("dve", lambda e: e.scalar_tensor_tensor(out=out, in0=in0, scalar=scalar, in1=in1, op0=op0, op1=op1), r, w)

    def tsc(out, in0, s1, s2, op0, op1, r=(), w=()):
        S.add("dve", lambda e: e.tensor_scalar(out=out, in0=in0, scalar1=s1, scalar2=s2, op0=op0, op1=op1), r, w)

    def tsc1(out, in0, s1, op0, r=(), w=()):
        S.add("dve", lambda e: e.tensor_scalar(out=out, in0=in0, scalar1=s1, scalar2=None, op0=op0), r, w)

    def rsqrt_(ap, key):
        act(ap, ap, AF.Sqrt, r=[key], w=[key])
        S.add("dve", lambda e: e.reciprocal(out=ap, in_=ap), [key], [key])

    def mm(out, lhsT, rhs, start, stop, r=(), w=()):
        S.add("pe", lambda e: e.matmul(out, lhsT, rhs, start=start, stop=stop), r, w)

    def tr(out, in_, ident, r=(), w=()):
        S.add("pe", lambda e: e.transpose(out, in_, ident), r, w)

    def memset(ap, val, w=()):
        S.add("dve", lambda e: e.memset(ap, val), (), w)

    modT = sb("modT", [128, 12 * NCD])
    wm = sb("wm", [128, 5 * NCD])
    zshift = sb("zshift", [128, NCD])
    ones = sb("ones", [128, 128])
    idb = sb("idb", [128, 128], BF16)
    wbuf = [sb(f"wbuf{i}", [128, NCD * 128], BF16) for i in range(3)]
    wctr = [0]
    memset(ones[:], 1.0, w=["ones"])
    ones_b = sb("ones_b", [128, 128], BF16)
    memset(ones_b[:], 1.0, w=["ones"])
    memset(zshift[:], 0.0, w=["zshift"])
    dma("sp", idb[:], identb[:, :], w=["idb"])
    for c in range(NCD):
        dma("sp", xT[c * 128:(c + 1) * 128, :], xT0[c * 128:(c + 1) * 128, :], w=[("xT", c)])

    def MOD(l, m, c=None):
        base = (l * 6 + m) * NCD
        return modT[:, base:base + NCD] if c is None else modT[:, base + c:base + c + 1]

    def proj_ws(wd, blocks, KC, rhs_pieces, consume, rkeys, prepare=None):
        for bi, blk in enumerate(blocks):
            wi = wctr[0] % 3
            wctr[0] += 1
            wb = wbuf[wi]
            dma("pq", wb[:, 0:KC * 128], wd[blk], w=[("wbuf", wi)])
            bks = [nbank() for _ in rhs_pieces]
            if prepare is not None:
                for pi in range(len(rhs_pieces)):
                    prepare(bi, pi)
            for k in range(KC):
                for pi, (fn, n) in enumerate(rhs_pieces):
                    b = bks[pi]
                    mm(ps[b][:, 0:n], wb[:, k * 128:(k + 1) * 128], fn(k), k == 0, k == KC - 1,
                       r=[("wbuf", wi)] + list(rkeys), w=[("ps", b)])
            for pi, (fn, n) in enumerate(rhs_pieces):
                consume(bi, pi, bks[pi], n)

    def norm_mod(st, col0, ncols, wmod, shift, dst, dkey, out_dram=None):
        xa = sb(f"nm_xa_{dkey}", [128, NCD, 512], F32, st)
        sq = [sb(f"nm_sq{i}_{dkey}", [128, 512], BF16, st) for i in range(4)]
        rstd = sb(f"nm_rstd_{dkey}", [128, 512], F32, st)
        tmp = [sb(f"nm_tmp{i}_{dkey}", [128, 512], F32, st) for i in range(2)]
        ob = [sb(f"nm_ob{i}_{dkey}", [128, 512], F32, st) for i in range(2)] if out_dram is not None else None
        NQ = 4 if NCD % 4 == 0 else 1
        CQ = NCD // NQ
        for (c0, n) in pieces(ncols):
            for qd in range(NQ):
                dma("sp", xa[:, qd * CQ:(qd + 1) * CQ, 0:n],
                    xT[qd * CQ * 128:(qd + 1) * CQ * 128, col0 + c0:col0 + c0 + n].rearrange("(c p) t -> p c t", p=128),
                    r=[("xT", c) for c in range(qd * CQ, (qd + 1) * CQ)], w=[("nmxa", qd)])
            b = nbank()
            for c in range(NCD):
                i4 = c % 4
                act(sq[i4][:, 0:n], xa[:, c, 0:n], AF.Square, r=[("nmxa", c // CQ)], w=[("nmsq", i4)])
                mm(ps[b][:, 0:n], ones_b[:], sq[i4][:, 0:n], c == 0, c == NCD - 1, r=[("nmsq", i4), "ones"], w=[("ps", b)])
            tsc(rstd[:, 0:n], ps[b][:, 0:n], 1.0 / D, EPS, ALU.mult, ALU.add, r=[("ps", b)], w=["nmrstd"])
            rsqrt_(rstd[:, 0:n], "nmrstd")
            for c in range(NCD):
                i2 = c % 2
                tt(tmp[i2][:, 0:n], xa[:, c, 0:n], rstd[:, 0:n], ALU.mult, r=[("nmxa", c // CQ), "nmrstd"], w=[("nmtmp", i2)])
                if out_dram is None:
                    act(dst(c, c0, n), tmp[i2][:, 0:n], AF.Identity, scale=wmod[:, c:c + 1], bias=shift[:, c:c + 1],
                        r=[("nmtmp", i2), "mod"], w=[dkey])
                else:
                    act(ob[i2][:, 0:n], tmp[i2][:, 0:n], AF.Identity, scale=wmod[:, c:c + 1], bias=shift[:, c:c + 1],
                        r=[("nmtmp", i2), "mod"], w=[("nmob", i2)])
                    dma("sp", out_dram[c * 128:(c + 1) * 128, col0 + c0:col0 + c0 + n], ob[i2][:, 0:n], r=[("nmob", i2)], w=[("yT", c)])

    def make_resid(st, gate, col0, tag):
        xr = [sb(f"xr{i}_{tag}", [128, 512], F32, st) for i in range(6)]
        ctr = [0]
        slot = {}

        def prepare(bi, pi):
            j = bi
            c0, n = pieces(TP)[pi]
            i = ctr[0] % 6
            ctr[0] += 1
            slot[(bi, pi)] = i
            dma("sp", xr[i][:, 0:n], xT[j * 128:(j + 1) * 128, col0 + c0:col0 + c0 + n], r=[("xT", j)], w=[("xr", i)])

        def consume(bi, pi, b, n):
            j = bi
            c0 = pieces(TP)[pi][0]
            i = slot.pop((bi, pi))
            dst = xT[j * 128:(j + 1) * 128, col0 + c0:col0 + c0 + n]
            stt(xr[i][:, 0:n], ps[b][:, 0:n], gate[:, j:j + 1], xr[i][:, 0:n], ALU.mult, ALU.add, r=[("ps", b), ("xr", i), "mod"], w=[("xr", i)])
            dma("sp", dst, xr[i][:, 0:n], r=[("xr", i)], w=[("xT", j)])
        return prepare, consume

    with contextlib.ExitStack() as st:
        cv = sb("cv", [128, NCD], F32, st)
        scT = sb("scT", [128, NCD], BF16, st)
        abT = sb("abT", [128, 12 * NCD], F32, st)
        nw = sb("nw", [128, 5 * NCD], F32, st)
        dma("sp", cv[:], cvec[:, :], w=["cv"])
        dma("sp", abT[:], ada_bT[:, :], w=["abT"])
        dma("sp", nw[:, 0:2 * NCD], n1T[:, :], w=["nw"])
        dma("sp", nw[:, 2 * NCD:4 * NCD], n2T[:, :], w=["nw"])
        dma("sp", nw[:, 4 * NCD:5 * NCD], nfT[:, :], w=["nw"])
        act(scT[:], cv[:], AF.Silu, r=["cv"], w=["scT"])

        def cons_ada(bi, pi, b, n):
            tt(modT[:, bi:bi + 1], ps[b][:, 0:1], abT[:, bi:bi + 1], ALU.add, r=[("ps", b), "abT"], w=["mod"])
        proj_ws(ada_wb, list(range(12 * NCD)), NCD, [(lambda k: scT[:, k:k + 1], 1)], cons_ada, ["scT"])
        for l in range(2):
            stt(wm[:, (2 * l) * NCD:(2 * l + 1) * NCD], MOD(l, 1), 1.0, nw[:, l * NCD:(l + 1) * NCD], ALU.add, ALU.mult, r=["mod", "nw"], w=["mod"])
            stt(wm[:, (2 * l + 1) * NCD:(2 * l + 2) * NCD], MOD(l, 4), 1.0, nw[:, (2 + l) * NCD:(3 + l) * NCD], ALU.add, ALU.mult, r=["mod", "nw"], w=["mod"])
        S.add("dve", lambda e: e.tensor_copy(out=wm[:, 4 * NCD:5 * NCD], in_=nw[:, 4 * NCD:5 * NCD]), ["nw"], ["mod"])
    S.barrier()

    if getattr(cfg, "STOP", 99) <= 0:
        S.emit(); es0.close(); return nc, names_in
    def conv3(o, ub, t1, w0, w1, w2, bias, msp, mep, rk, wk):
        if bias is None:
            act(o[:, :], ub[:, 1:TP + 1], AF.Identity, scale=w1, r=rk + ["cw"], w=[wk])
        else:
            act(o[:, :], ub[:, 1:TP + 1], AF.Identity, scale=w1, bias=bias, r=rk + ["cw"], w=[wk])
        tt(t1[:, :], ub[:, 0:TP], msp, ALU.mult, r=rk + ["masks"], w=["cv_t1"])
        stt(o[:, :], t1[:, :], w0, o[:, :], ALU.mult, ALU.add, r=["cv_t1", wk, "cw"], w=[wk])
        tt(t1[:, :], ub[:, 2:TP + 2], mep, ALU.mult, r=rk + ["masks"], w=["cv_t1"])
        stt(o[:, :], t1[:, :], w2, o[:, :], ALU.mult, ALU.add, r=["cv_t1", wk, "cw"], w=[wk])

    def evac_ub(ub, p, ukey):
        pcs = pieces(TP)

        def f(pi, b, n):
            if pi < len(pcs):
                c0 = pcs[pi][0]
                act(ub[:, 1 + c0:1 + c0 + n], ps[b][:, 0:n], AF.Copy, r=[("ps", b)], w=[ukey])
            else:
                col = TP + 1 if p == 0 else 0
                act(ub[:, col:col + 1], ps[b][:, 0:1], AF.Copy, r=[("ps", b)], w=[ukey])
        return f

    def load_masks(st, p):
        msp = sb(f"msp{p}", [128, TP], F32, st)
        mep = sb(f"mep{p}", [128, TP], F32, st)
        dma("sp", msp[:], mS[:, p * TP:(p + 1) * TP], w=["masks"])
        dma("sp", mep[:], mE[:, p * TP:(p + 1) * TP], w=["masks"])
        return msp, mep

    def make_hT(st, p, wmod, shift, tag, hx=None):
        hT = sb(f"hT_{tag}", [128, NCD, TP + 1], BF16, st)
        with contextlib.ExitStack() as st2:
            norm_mod(st2, p * TP, TP, wmod, shift, lambda c, c0, n: hT[:, c, c0:c0 + n], "hT")
        S.barrier()
        if hx is None:
            with contextlib.ExitStack() as st2:
                xcol = TP if p == 0 else TP - 1
                norm_mod(st2, xcol, 1, wmod, shift, lambda c, c0, n: hT[:, c, TP:TP + 1], "hT")
            S.barrier()
        else:
            S.add("dve", lambda e: e.tensor_copy(out=hT[:, :, TP:TP + 1], in_=hx[:, :, p:p + 1]), ["hx"], ["hT"])
        return hT

    def make_hx(st, wmod, shift):
        hx = sb("hx", [128, NCD, 2], BF16, st)
        for p in range(2):
            with contextlib.ExitStack() as st2:
                xcol = TP if p == 0 else TP - 1
                norm_mod(st2, xcol, 1, wmod, shift, lambda c, c0, n, p=p: hx[:, c, p:p + 1], "hx")
            S.barrier()
        return hx

    for p in range(2):
        with contextlib.ExitStack() as st:
            hT = make_hT(st, p, wm[:, 0:NCD], MOD(0, 0), f"h1_{p}")
            msp, mep = load_masks(st, p)
            cw = sb("hy_cw_s", [128, 9 * NCD], F32, st)
            sbv = sb("hy_sb_s", [128, 3 * NCD], F32, st)
            dma("sp", cw[:], hy_cw[:, :], w=["cw"])
            dma("sp", sbv[:], hy_sb[:, :], w=["cw"])
            ub = [[sb(f"ub{i}_{w_}", [128, TP + 2], F32, st) for w_ in range(3)] for i in range(2)]
            oo = [sb(f"oo{w_}", [128, TP], F32, st) for w_ in range(3)]
            t1 = sb("cv_t1", [128, TP], F32, st)
            zb = [[sb(f"zb{i}_{w_}", [128, TP], BF16, st) for w_ in range(2)] for i in range(2)]
            tk = [sb(f"tk{i}", [128, GP, 128], BF16, st) for i in range(2)]
            for i in range(2):
                for w_ in range(3):
                    memset(ub[i][w_][:, :], 0.0, w=[("ub", i, w_)])
            rp = [(lambda k, c0=c0, n=n: hT[:, k, c0:c0 + n], n) for (c0, n) in pieces(TP)] + [(lambda k: hT[:, k, TP:TP + 1], 1)]

            def finish_h1(i, j):
                for which, key in ((0, "x0"), (1, "z")):
                    tb = ntbank()
                    for g in range(GP):
                        tr(pst[tb][:, g * 128:(g + 1) * 128], zb[i][which][:, g * 128:(g + 1) * 128], idb[:], r=[("zb", i, which), "idb"], w=[("pst", tb)])
                    S.add("act", lambda e, tb=tb, which=which: e.activation(out=tk[which][:, :, :], in_=pst[tb][:, 0:GP * 128].rearrange("p (g c) -> p g c", g=GP), func=AF.Copy),
                          [("pst", tb)], [("tk", which)])
                    dst = (x0tok if which == 0 else ztok)[p * TP:(p + 1) * TP, j * 128:(j + 1) * 128].rearrange("(g q) c -> q g c", q=128)
                    dma("sp", dst, tk[which][:, :, :], r=[("tk", which)], w=[key + "tok"])
            pend = None
            for j in range(NCD):
                i = j % 2
                for w_ in range(3):
                    ev = evac_ub(ub[i][w_], p, ("ub", i, w_))
                    proj_ws(hy_win, [j * 3 + w_], NCD, rp, lambda bi, pi, b, n, ev=ev: ev(pi, b, n), ["hT"])
                for w_ in range(3):
                    idx = w_ * NCD + j
                    conv3(oo[w_], ub[i][w_], t1, cw[:, idx:idx + 1], cw[:, 3 * NCD + idx:3 * NCD + idx + 1],
                          cw[:, 6 * NCD + idx:6 * NCD + idx + 1], sbv[:, idx:idx + 1], msp[:, :], mep[:, :],
                          [("ub", i, w_)], ("oo", w_))
                act(zb[i][0][:, :], oo[0][:, :], AF.Copy, r=[("oo", 0)], w=[("zb", i, 0)])
                tt(zb[i][1][:, :], oo[2][:, :], oo[1][:, :], ALU.mult, r=[("oo", 1), ("oo", 2)], w=[("zb", i, 1)])
                if pend is not None:
                    finish_h1(*pend)
                pend = (i, j)
            finish_h1(*pend)
        S.barrier()

    if getattr(cfg, "STOP", 99) <= 1:
        S.emit(); es0.close(); return nc, names_in
    with contextlib.ExitStack() as st:
        w1s = sb("w1s", [33, 64], F32, st); w2s = sb("w2s", [64, 64], F32, st); w3s = sb("w3s", [64, 64], F32, st)
        bfs = sb("bfs", [64, 6], F32, st)
        sc_ = sb("fsc", [64, 12], F32, st)
        wos = sb("wos", [64, 2, 512], F32, st)
        hA = sb("hA", [64, TS], F32, st)
        ngt = sb("ngt", [128, NG], F32, st); nfs = sb("nfs", [128, NG], F32, st)
        nss = sb("nss", [128, NG], F32, st); sps = sb("sps", [128, NG], F32, st)
        stm = contextlib.ExitStack()
        ze = sb("ze", [33, TS], F32, stm)
        hB = sb("hB", [64, TS], F32, stm)
        s2 = sb("s2", [64, 512], F32, stm); s4 = sb("s4", [64, 512], F32, stm)
        for (dst_, src_) in ((w1s, hy_w1), (w2s, hy_w2), (w3s, hy_w3), (bfs, hy_bf), (ze, zembT),
                             (ngt, negt), (nfs, nfirst), (nss, nsT), (sps, spT)):
            dma("sp", dst_[:], src_[:, :], w=["fconst"])
        for i in range(3):
            tt(sc_[:, 4 * i + 1:4 * i + 2], bfs[:, 3 + i:4 + i], bfs[:, i:i + 1], ALU.mult, r=["fconst"], w=["fsc"])
            tsc1(sc_[:, 4 * i + 3:4 * i + 4], sc_[:, 4 * i + 1:4 * i + 2], 0.25, ALU.mult, r=["fsc"], w=["fsc"])
            tsc1(sc_[:, 4 * i + 1:4 * i + 2], sc_[:, 4 * i + 1:4 * i + 2], 0.5, ALU.mult, r=["fsc"], w=["fsc"])
            tsc1(sc_[:, 4 * i:4 * i + 1], bfs[:, 3 + i:4 + i], 0.5, ALU.mult, r=["fconst"], w=["fsc"])
            tsc1(sc_[:, 4 * i + 2:4 * i + 3], bfs[:, 3 + i:4 + i], 0.25, ALU.mult, r=["fconst"], w=["fsc"])
        srcs = [(w1s, ze, 33), (w2s, hA, 64), (w3s, hB, 64)]
        dsts = [hA, hB, hA]
        for i in range(3):
            wS, src, kk = srcs[i]
            dstT = dsts[i]
            for (c0, n) in pieces(TS):
                b = nbank()
                mm(ps[b][0:64, 0:n], wS[0:kk, :], src[0:kk, c0:c0 + n], True, True, r=["fconst", ("hmlp", i)], w=[("ps", b)])
                act(s2[:, 0:n], ps[b][0:64, 0:n], AF.Sin, scale=sc_[:, 4 * i:4 * i + 1], bias=sc_[:, 4 * i + 1:4 * i + 2], r=[("ps", b), "fsc"], w=["s2"])
                act(s4[:, 0:n], ps[b][0:64, 0:n], AF.Sin, scale=sc_[:, 4 * i + 2:4 * i + 3], bias=sc_[:, 4 * i + 3:4 * i + 4], r=[("ps", b), "fsc"], w=["s4"])
                tt(s4[:, 0:n], s4[:, 0:n], s4[:, 0:n], ALU.mult, r=["s4"], w=["s4"])
                tsc(s4[:, 0:n], s4[:, 0:n], -2.0, 1.0, ALU.mult, ALU.add, r=["s4"], w=["s4"])
                stt(dstT[:, c0:c0 + n], s2[:, 0:n], 2.0, s4[:, 0:n], ALU.mult, ALU.mult, r=["s2", "s4"], w=[("hmlp", i + 1)])
        h3T = hA
        stm.close()
        S.barrier()
        hsum = sb("hsum", [128, NG, 512], BF16, st); hdif = sb("hdif", [128, NG, 512], BF16, st)
        zs = sb("zs", [128, NG, 512], BF16, st); x0g = [sb(f"x0g{i}", [128, 512], BF16, st) for i in range(2)]
        Yb = sb("Yb", [128, NCC, 512], BF16, st)
        dlb = sb("dlb", [128, 512], F32, st); fbb = sb("fbb", [128, 512], F32, st)
        dec = sb("dec", [128, 512], F32, st); hf_ = sb("hf_", [128, 512], F32, st); hb_ = sb("hb_", [128, 512], F32, st)
        Fb = [[sb(f"Fb{i}_{k}", [128, NG * 128], BF16, st) for k in range(4)] for i in range(2)]
        KA = sb("KA", [128, 512], F32, st); KB = sb("KB", [128, 512], F32, st); KC = sb("KC", [128, 512], F32, st)
        c1 = sb("c1", [128, 512], F32, st); c2 = sb("c2", [128, 512], F32, st)
        Gb = [sb(f"Gb{i}", [128, NCC * 128], BF16, st) for i in range(2)]
        gt = sb("gt", [128, 512], F32, st); gbf = sb("gbf", [128, 512], BF16, st)
        gtr = [sb(f"gtr{i}", [128, 4, 128], BF16, st) for i in range(2)]
        for cb in range(NCB):
            cs = slice(cb * 512, (cb + 1) * 512)
            dma("sp", dlb[:], deltab[:, cs], w=["dlb"])
            dma("sp", fbb[:], hy_fb[:, cs], w=["fbb"])
            dma("sp", zs[:, :, :], ztok[:, cs].rearrange("(g q) c -> q g c", q=128), r=["ztok"], w=["zs"])
            dma("sp", wos[:, 0, :], hy_wo[:, cb * 512:(cb + 1) * 512], w=["wos"])
            dma("sp", wos[:, 1, :], hy_wo[:, D + cb * 512:D + (cb + 1) * 512], w=["wos"])
            for jc in range(NG):
                bf_, bb_ = nbank(), nbank()
                mm(ps[bf_][:, :], h3T[:, jc * 128:(jc + 1) * 128], wos[:, 0, :], True, True, r=[("hmlp", 3), "wos"], w=[("ps", bf_)])
                mm(ps[bb_][:, :], h3T[:, jc * 128:(jc + 1) * 128], wos[:, 1, :], True, True, r=[("hmlp", 3), "wos"], w=[("ps", bb_)])
                act(dec[:], dlb[:], AF.Exp, scale=ngt[:, jc:jc + 1], r=["dlb", "fconst"], w=["dec"])
                tt(hf_[:], ps[bf_][:, :], dec[:], ALU.mult, r=[("ps", bf_), "dec"], w=["hf_"])
                stt(hb_[:], ps[bb_][:, :], nfs[:, jc:jc + 1], dec[:], ALU.mult, ALU.mult, r=[("ps", bb_), "dec", "fconst"], w=["hb_"])
                tt(hsum[:, jc, :], hf_[:], hb_[:], ALU.add, r=["hf_", "hb_"], w=["hsum"])
                tt(hdif[:, jc, :], hf_[:], hb_[:], ALU.subtract, r=["hf_", "hb_"], w=["hdif"])
            for i in range(NG):
                fi = i % 2
                special = (i % cfg.CPS == 0)
                dma("sp", Fb[fi][0][:], Fre[i], w=[("Fb", fi, 0)])
                dma("sp", Fb[fi][1][:], Fim[i], w=[("Fb", fi, 1)])
                dma("sp", Fb[fi][2][:], Fzi[i], w=[("Fb", fi, 2)])
                if special:
                    dma("sp", Fb[fi][3][:], Fn[i], w=[("Fb", fi, 3)])
                bkr, bki, bzr, bzi = nbank(), nbank(), nbank(), nbank()
                for jc in range(NG):
                    mm(ps[bkr][:, :], Fb[fi][0][:, jc * 128:(jc + 1) * 128], hsum[:, jc, :], jc == 0, jc == NG - 1, r=[("Fb", fi, 0), "hsum"], w=[("ps", bkr)])
                for jc in range(NG):
                    mm(ps[bki][:, :], Fb[fi][1][:, jc * 128:(jc + 1) * 128], hdif[:, jc, :], jc == 0, (jc == NG - 1) and not special, r=[("Fb", fi, 1), "hdif"], w=[("ps", bki)])
                if special:
                    for jc in range(NG):
                        mm(ps[bki][:, :], Fb[fi][3][:, jc * 128:(jc + 1) * 128], hsum[:, jc, :], False, jc == NG - 1, r=[("Fb", fi, 3), "hsum"], w=[("ps", bki)])
                for jc in range(NG):
                    mm(ps[bzr][:, :], Fb[fi][0][:, jc * 128:(jc + 1) * 128], zs[:, jc, :], jc == 0, jc == NG - 1, r=[("Fb", fi, 0), "zs"], w=[("ps", bzr)])
                for jc in range(NG):
                    mm(ps[bzi][:, :], Fb[fi][2][:, jc * 128:(jc + 1) * 128], zs[:, jc, :], jc == 0, jc == NG - 1, r=[("Fb", fi, 2), "zs"], w=[("ps", bzi)])
                act(KA[:], ps[bkr][:, :], AF.Copy, r=[("ps", bkr)], w=["KA"])
                act(KB[:], ps[bki][:, :], AF.Identity, scale=nss[:, i:i + 1], r=[("ps", bki), "fconst"], w=["KB"])
                act(c1[:], ps[bki][:, :], AF.Identity, scale=sps[:, i:i + 1], r=[("ps", bki), "fconst"], w=["c1"])
                stt(KC[:], KA[:], nss[:, i:i + 1], c1[:], ALU.mult, ALU.add, r=["KA", "c1", "fconst"], w=["KC"])
                tt(c1[:], ps[bzr][:, :], KA[:], ALU.mult, r=[("ps", bzr), "KA", "c1"], w=["c1"])
                tt(c2[:], ps[bzi][:, :], KB[:], ALU.mult, r=[("ps", bzi), "KB"], w=["c2"])
                tt(Yb[:, i, :], c1[:], c2[:], ALU.subtract, r=["c1", "c2"], w=["Yb"])
                tt(c1[:], ps[bzr][:, :], KB[:], ALU.mult, r=[("ps", bzr), "KB"], w=["c1"])
                tt(c2[:], ps[bzi][:, :], KC[:], ALU.mult, r=[("ps", bzi), "KC"], w=["c2"])
                tt(Yb[:, NG + i, :], c1[:], c2[:], ALU.add, r=["c1", "c2"], w=["Yb"])
            for g in range(NG):
                gi = g % 2
                dma("sp", Gb[gi][:], Gm[g], w=[("Gb", gi)])
                b = nbank()
                for i in range(NCC):
                    mm(ps[b][:, :], Gb[gi][:, i * 128:(i + 1) * 128], Yb[:, i, :], i == 0, i == NCC - 1, r=[("Gb", gi), "Yb"], w=[("ps", b)])
                tt(gt[:], zs[:, g, :], fbb[:], ALU.mult, r=["zs", "fbb"], w=["gt"])
                tt(gt[:], ps[b][:, :], gt[:], ALU.add, r=[("ps", b), "gt"], w=["gt"])
                dma("sp", x0g[gi][:], x0tok[g * 128:(g + 1) * 128, cs], r=["x0tok"], w=[("x0g", gi)])
                tt(gbf[:], gt[:], x0g[gi][:], ALU.mult, r=["gt", ("x0g", gi)], w=["gbf"])
                tb = ntbank()
                for cc in range(4):
                    tr(pst[tb][:, cc * 128:(cc + 1) * 128], gbf[:, cc * 128:(cc + 1) * 128], idb[:], r=["gbf", "idb"], w=[("pst", tb)])
                S.add("act", lambda e, tb=tb, gi=gi: e.activation(out=gtr[gi][:, :, :], in_=pst[tb][:, 0:512].rearrange("p (c t) -> p c t", c=4), func=AF.Copy),
                      [("pst", tb)], [("gtr", gi)])
                dma("sp", gT[cb * 512:(cb + 1) * 512, g * 128:(g + 1) * 128].rearrange("(c q) t -> q c t", q=128), gtr[gi][:, :, :], r=[("gtr", gi)], w=["gT"])
    S.barrier()

    if getattr(cfg, "STOP", 99) <= 2:
        S.emit(); es0.close(); return nc, names_in
    def out_proj(wd, nhalf, gate, tag):
        for p in range(2):
            for hf in range(nhalf):
                with contextlib.ExitStack() as st:
                    aT = sb(f"aT_{tag}", [128, NCD, TP], BF16, st)
                    for c in range(NCD):
                        dma("sp", aT[:, c, :], gT[(hf * NCD + c) * 128:(hf * NCD + c + 1) * 128, p * TP:(p + 1) * TP], r=["gT"], w=["aT"])
                    prep, cons = make_resid(st, gate, p * TP, tag)
                    rp = [(lambda k, c0=c0, n=n: aT[:, k, c0:c0 + n], n) for (c0, n) in pieces(TP)]
                    wdd = wd if nhalf == 1 else wd[hf]
                    proj_ws(wdd, list(range(NCD)), NCD, rp, cons, ["aT"], prepare=prep)
                S.barrier()

    out_proj(hy_wout, 1, MOD(0, 2), "h3")

    if getattr(cfg, "STOP", 99) <= 3:
        S.emit(); es0.close(); return nc, names_in
    def ffn(l):
        sthx = contextlib.ExitStack()
        hx = make_hx(sthx, wm[:, (2 * l + 1) * NCD:(2 * l + 2) * NCD], MOD(l, 3))
        for p in range(2):
            with contextlib.ExitStack() as st:
                hT = make_hT(st, p, wm[:, (2 * l + 1) * NCD:(2 * l + 2) * NCD], MOD(l, 3), f"f{l}_{p}", hx=hx)
                msp, mep = load_masks(st, p)
                cw = sb("ffn_cw_s", [128, 6 * NFF], F32, st)
                dma("sp", cw[:], ffn_cw[l][:, :], w=["cw"])
                ub = [[sb(f"fub{i}_{w_}", [128, TP + 2], F32, st) for w_ in range(2)] for i in range(2)]
                oo = [[sb(f"foo{i}_{w_}", [128, TP], F32, st) for w_ in range(2)] for i in range(2)]
                t1 = sb("fcv_t1", [128, TP], F32, st)
                sl = sb("fsl", [128, TP], F32, st)
                emax = max(cfg.ESZ)
                actT = sb("actT", [128, emax, TP], BF16, st)
                prep, cons = make_resid(st, MOD(l, 5), p * TP, f"f{l}")
                for i in range(2):
                    for w_ in range(2):
                        memset(ub[i][w_][:, :], 0.0, w=[("ub", i, w_)])
                rp = [(lambda k, c0=c0, n=n: hT[:, k, c0:c0 + n], n) for (c0, n) in pieces(TP)] + [(lambda k: hT[:, k, TP:TP + 1], 1)]

                def finish(i, jj):
                    act(sl[:, :], oo[i][0][:, :], AF.Silu, r=[("oo", i, 0)], w=["fsl"])
                    tt(actT[:, jj, :], sl[:, :], oo[i][1][:, :], ALU.mult, r=["fsl", ("oo", i, 1)], w=["actT"])
                j0 = 0
                for e in range(cfg.NE):
                    pend = None
                    for jj in range(cfg.ESZ[e]):
                        j = j0 + jj
                        i = j % 2
                        for w_ in range(2):
                            ev = evac_ub(ub[i][w_], p, ("ub", i, w_))
                            proj_ws(ffn_up[l], [w_ * NFF + j], NCD, rp, lambda bi, pi, b, n, ev=ev: ev(pi, b, n), ["hT"])
                        for w_ in range(2):
                            idx = w_ * NFF + j
                            conv3(oo[i][w_], ub[i][w_], t1, cw[:, idx:idx + 1], cw[:, 2 * NFF + idx:2 * NFF + idx + 1],
                                  cw[:, 4 * NFF + idx:4 * NFF + idx + 1], None, msp[:, :], mep[:, :], [("ub", i, w_)], ("oo", i, w_))
                        if pend is not None:
                            finish(*pend)
                        pend = (i, jj)
                    finish(*pend)
                    rpd = [(lambda k, c0=c0, n=n: actT[:, k, c0:c0 + n], n) for (c0, n) in pieces(TP)]
                    proj_ws(ffn_dn[l][e], list(range(NCD)), cfg.ESZ[e], rpd, cons, ["actT"], prepare=prep)
                    j0 += cfg.ESZ[e]
            S.barrier()
        sthx.close()
        S.barrier()

    ffn(0)

    if getattr(cfg, "STOP", 99) <= 4:
        S.emit(); es0.close(); return nc, names_in
    lg = sb("lg", [128, 2 * NH])
    kdec = sb("kdec", [128, 2 * NH])
    g128 = sb("g128", [128, 2 * NH])
    MT = sb("MT", [128, NH, 128])
    rt = sb("rt", [128, 7 * 128])
    cms = sb("cms", [128, 2 * NG])
    with contextlib.ExitStack() as st:
        kp = sb("kp", [128, 2 * NH], F32, st)
        e1 = sb("e1", [128, 128], F32, st); e2 = sb("e2", [128, 128], F32, st)
        dma("sp", lg[:], ret_lg[:, :], w=["lg"])
        dma("sp", rt[:], rtab[:, :], w=["rt"])
        dma("sp", kp[:], kpos[:, :], w=["kp"])
        dma("sp", cms[:, 0:NG], cmf[:, :], w=["cms"])
        dma("sp", cms[:, NG:2 * NG], cmb[:, :], w=["cms"])
        act(lg[:], lg[:], AF.Exp, scale=-1.0, r=["lg"], w=["lg"])
        act(lg[:], lg[:], AF.Ln, bias=1.0, r=["lg"], w=["lg"])
        tsc1(lg[:], lg[:], -1.0, ALU.mult, r=["lg"], w=["lg"])
        tt(kdec[:], kp[:], lg[:], ALU.mult, r=["kp", "lg"], w=["kdec"])
        act(kdec[:], kdec[:], AF.Exp, r=["kdec"], w=["kdec"])
        tsc1(kdec[:], kdec[:], 0.0625, ALU.mult, r=["kdec"], w=["kdec"])
        act(g128[:], lg[:], AF.Exp, scale=128.0, r=["lg"], w=["g128"])
        for h in range(NH):
            act(e1[:], rt[:, 0:128], AF.Exp, scale=lg[:, h:h + 1], r=["rt", "lg"], w=["e1"])
            act(e2[:], rt[:, 128:256], AF.Exp, scale=lg[:, NH + h:NH + h + 1], r=["rt", "lg"], w=["e2"])
            tt(e1[:], e1[:], rt[:, 256:384], ALU.mult, r=["e1", "rt"], w=["e1"])
            tt(e2[:], e2[:], rt[:, 384:512], ALU.mult, r=["e2", "rt"], w=["e2"])
            tt(e1[:], e1[:], e2[:], ALU.add, r=["e1", "e2"], w=["e1"])
            tt(e1[:], e1[:], rt[:, 512:640], ALU.add, r=["e1", "rt"], w=["e1"])
            tsc1(MT[:, h, :], e1[:], 0.0625, ALU.mult, r=["e1"], w=["MT"])
    S.barrier()

    if getattr(cfg, "STOP", 99) == 45:
        S.emit(); es0.close(); return nc, names_in
    for p in range(2):
        with contextlib.ExitStack() as st:
            hT = sb("hT_r", [128, NCD, TP], BF16, st)
            with contextlib.ExitStack() as st2:
                norm_mod(st2, p * TP, TP, wm[:, 2 * NCD:3 * NCD], MOD(1, 0), lambda c, c0, n: hT[:, c, c0:c0 + n], "hT")
            S.barrier()
            tabs = [sb(f"rope{i}", [128, TP], F32, st) for i in range(2)]
            for i, src in enumerate((cosq, sinq)):
                dma("sp", tabs[i][:], src[:, p * TP:(p + 1) * TP], w=["rope"])
            qk = [sb(f"qkb{i}", [128, 2, TP], BF16, st) for i in range(2)]
            r1 = sb("r1", [128, 512], F32, st); r2 = sb("r2", [128, 512], F32, st)
            kfb = sb("kfb", [128, GP, 256], BF16, st); kbb = sb("kbb", [128, GP, 256], BF16, st)
            wbig = [sb(f"wbig{i}", [128, NCD * 256], BF16, st) for i in range(2)]
            vb = [sb(f"vb{i}", [128, 512], BF16, st) for i in range(2)]
            gnb = sb("gnb", [128, 512], F32, st)
            sgt = sb("sgt", [128, 512], F32, st)
            pcs = pieces(TP)
            rp = [(lambda k, c0=c0, n=n: hT[:, k, c0:c0 + n], n) for (c0, n) in pcs]
            vctr = 0
            vctr2 = [0]
            for h in range(NH):
                for qi, (ct, sn, dd) in enumerate(((tabs[0], tabs[1], qT_d), (tabs[0], tabs[1], kT_d))):
                    banks = {}

                    def cons(bi, pi, b, n, banks=banks, qi=qi, ct=ct, sn=sn):
                        banks[(bi, pi)] = b
                        if bi == 1:
                            c0 = pcs[pi][0]
                            ba, bb = banks[(0, pi)], b
                            tt(r1[:, 0:n], ps[ba][:, 0:n], ct[:, c0:c0 + n], ALU.mult, r=[("ps", ba), "rope"], w=["r1"])
                            tt(r2[:, 0:n], ps[bb][:, 0:n], sn[:, c0:c0 + n], ALU.mult, r=[("ps", bb), "rope"], w=["r2"])
                            tt(qk[qi][:, 0, c0:c0 + n], r1[:, 0:n], r2[:, 0:n], ALU.subtract, r=["r1", "r2"], w=[("qk", qi)])
                            tt(r1[:, 0:n], ps[ba][:, 0:n], sn[:, c0:c0 + n], ALU.mult, r=[("ps", ba), "rope"], w=["r1"])
                            tt(r2[:, 0:n], ps[bb][:, 0:n], ct[:, c0:c0 + n], ALU.mult, r=[("ps", bb), "rope"], w=["r2"])
                            tt(qk[qi][:, 1, c0:c0 + n], r1[:, 0:n], r2[:, 0:n], ALU.add, r=["r1", "r2"], w=[("qk", qi)])
                    proj_ws(ret_qk, [h * 4 + qi * 2, h * 4 + qi * 2 + 1], NCD, rp, cons, ["hT"])
                    for dc in range(2):
                        dma("sp", dd[(h * 2 + dc) * 128:(h * 2 + dc + 1) * 128, p * TP:(p + 1) * TP], qk[qi][:, dc, :], r=[("qk", qi)], w=["qkT_d"])
                for half in range(0 if getattr(cfg, "SKIP_KT", 0) else 2 * GP // 8 if 2 * GP >= 8 else 1):
                    tb = ntbank()
                    items = [(g, dc) for g in range(GP) for dc in range(2)][half * 8:(half + 1) * 8]
                    for ii, (g, dc) in enumerate(items):
                        tr(pst[tb][:, ii * 128:(ii + 1) * 128], qk[1][:, dc, g * 128:(g + 1) * 128], idb[:], r=[("qk", 1), "idb"], w=[("pst", tb)])
                    g0 = items[0][0]
                    ng_ = len(items) // 2
                    src_ap = lambda tb=tb, ng_=ng_: pst[tb][:, 0:ng_ * 256].rearrange("p (g c) -> p g c", g=ng_)
                    S.add("act", lambda e, src_ap=src_ap, g0=g0, ng_=ng_, h=h: e.activation(out=kfb[:, g0:g0 + ng_, :], in_=src_ap(), func=AF.Identity, scale=kdec[:, h:h + 1]),
                          [("pst", tb), "kdec"], ["kfb"])
                    S.add("act", lambda e, src_ap=src_ap, g0=g0, ng_=ng_, h=h: e.activation(out=kbb[:, g0:g0 + ng_, :], in_=src_ap(), func=AF.Identity, scale=kdec[:, NH + h:NH + h + 1]),
                          [("pst", tb), "kdec"], ["kbb"])
                dma("sp", kf_d[p * TP:(p + 1) * TP, h * 256:(h + 1) * 256].rearrange("(g q) c -> q g c", q=128), kfb[:, :, :], r=["kfb"], w=["kf_d"])
                dma("sp", kb_d[p * TP:(p + 1) * TP, h * 256:(h + 1) * 256].rearrange("(g q) c -> q g c", q=128), kbb[:, :, :], r=["kbb"], w=["kb_d"])
                dma("sp", gnb[:], ret_gn[:, h * 512:(h + 1) * 512], w=["gnb"])
                for which in range(0 if getattr(cfg, "SKIP_VG", 0) else 2):
                  for hv in range(2):
                    wi = vctr % 2
                    dma("pq", wbig[wi][:], ret_vg[(h * 2 + which) * 2 + hv], w=[("wbig", wi)])
                    for g in range(GP):
                        b = nbank()
                        for k in range(NCD):
                            mm(ps[b][:, 0:256], hT[:, k, g * 128:(g + 1) * 128], wbig[wi][:, k * 256:(k + 1) * 256], k == 0, k == NCD - 1,
                               r=["hT", ("wbig", wi)], w=[("ps", b)])
                        vi = (vctr2[0]) % 2
                        vctr2[0] += 1
                        rows = slice(p * TP + g * 128, p * TP + (g + 1) * 128)
                        cols = slice(h * 512 + hv * 256, h * 512 + (hv + 1) * 256)
                        if which == 0:
                            act(vb[vi][:, 0:256], ps[b][:, 0:256], AF.Copy, r=[("ps", b)], w=[("vb", vi)])
                            dma("sp", v_d[rows, cols], vb[vi][:, 0:256], r=[("vb", vi)], w=["v_d"])
                        else:
                            act(sgt[:, 0:256], ps[b][:, 0:256], AF.Silu, r=[("ps", b)], w=["sgt"])
                            tt(vb[vi][:, 0:256], sgt[:, 0:256], gnb[:, hv * 256:(hv + 1) * 256], ALU.mult, r=["sgt", "gnb"], w=[("vb", vi)])
                            dma("sp", sg_d[rows, cols], vb[vi][:, 0:256], r=[("vb", vi)], w=["sg_d"])
                    vctr += 1
        S.barrier()

    if getattr(cfg, "STOP", 99) <= 5:
        S.emit(); es0.close(); return nc, names_in
    with contextlib.ExitStack() as st:
        qTh = sb("qTh", [128, 2, TS], BF16, st); kTh = sb("kTh", [128, 2, TS], BF16, st)
        qfc = [sb(f"qfc{i}", [128, 2, 128], BF16, st) for i in range(2)]
        qbc = [sb(f"qbc{i}", [128, 2, 128], BF16, st) for i in range(2)]
        kfh = sb("kfh", [128, NG, 256], BF16, st); kbh = sb("kbh", [128, NG, 256], BF16, st)
        vh = sb("vh", [128, NG, 512], BF16, st); sgh = sb("sgh", [128, NG, 512], BF16, st)
        R = sb("Rst", [128, 2, 512], F32, st)
        Efc = [sb(f"Efc{i}", [128, 1024], BF16, st) for i in range(2)]
        Eb = sb("Eb", [128, NG, 1024], BF16, st)
        stg = [sb(f"stg{i}", [128, 2, 512], F32, st) for i in range(2)]
        cg = sb("cg", [128, 2 * NG], F32, st)
        qd = sb("qd", [128, 2, 256], F32, st)
        PT = [sb(f"PT{i}", [128, 128], BF16, st) for i in range(2)]
        ssum = sb("ssum", [128, 2], F32, st); junk = sb("junk", [128, 512], F32, st)
        gob = sb("gob", [128, 512], BF16, st)
        goT = sb("goT", [128, 4, TS], BF16, st)
        sctr = 0
        Rflat = R[:, :, :].rearrange("p a v -> p (a v)")
        for h in range(NH):
            for dc in range(2):
                dma("sp", qTh[:, dc, :], qT_d[(h * 2 + dc) * 128:(h * 2 + dc + 1) * 128, :], r=["qkT_d"], w=["qTh"])
                dma("sp", kTh[:, dc, :], kT_d[(h * 2 + dc) * 128:(h * 2 + dc + 1) * 128, :], r=["qkT_d"], w=["kTh"])
            dma("sp", kfh[:, :, :], kf_d[:, h * 256:(h + 1) * 256].rearrange("(g q) c -> q g c", q=128), r=["kf_d"], w=["kfh"])
            dma("sp", kbh[:, :, :], kb_d[:, h * 256:(h + 1) * 256].rearrange("(g q) c -> q g c", q=128), r=["kb_d"], w=["kbh"])
            dma("sp", vh[:, :, :], v_d[:, h * 512:(h + 1) * 512].rearrange("(g q) c -> q g c", q=128), r=["v_d"], w=["vh"])
            dma("sp", sgh[:, :, :], sg_d[:, h * 512:(h + 1) * 512].rearrange("(g q) c -> q g c", q=128), r=["sg_d"], w=["sgh"])
            tsc1(cg[:, 0:NG], cms[:, 0:NG], g128[:, h:h + 1], ALU.mult, r=["cms", "g128"], w=["cg"])
            tsc1(cg[:, NG:2 * NG], cms[:, NG:2 * NG], g128[:, NH + h:NH + h + 1], ALU.mult, r=["cms", "g128"], w=["cg"])
            for dc in range(2):
                act(qd[:, 0, dc * 128:(dc + 1) * 128], rt[:, 640:768], AF.Exp, scale=lg[:, h:h + 1], r=["rt", "lg"], w=["qd"])
                act(qd[:, 1, dc * 128:(dc + 1) * 128], rt[:, 768:896], AF.Exp, scale=lg[:, NH + h:NH + h + 1], r=["rt", "lg"], w=["qd"])

            def kv_update(kk, d_, c):
                nonlocal sctr
                cgo = d_ * NG
                for dc in range(2):
                    b = nbank()
                    mm(ps[b][:, :], kk[:, c, dc * 128:(dc + 1) * 128], vh[:, c, :], True, True, r=["vh", "kfh", "kbh"], w=[("ps", b)])
                    stt(R[:, dc, :], R[:, dc, :], cg[:, cgo + c:cgo + c + 1], ps[b][:, :], ALU.mult, ALU.add, r=["R", "cg", ("ps", b)], w=["R"])
                emit = ((c + 1) % cfg.CPS == 0) if d_ == 0 else (c % cfg.CPS == 0)
                if emit:
                    si = sctr % 2
                    sctr += 1
                    act(stg[si][:, :, :], R[:, :, :], AF.Copy, r=["R"], w=[("stg", si)])
                    dma("sp", st_out[d_, h, c // cfg.CPS].rearrange("(a q) v -> q a v", q=128), stg[si][:, :, :], r=[("stg", si)], w=["st_out"])

            dma("sp", R[:, :, :], s0[1, h].rearrange("(a q) v -> q a v", q=128), w=["R"])
            for c in range(NG - 1, -1, -1):
                act(Eb[:, c, :], Rflat, AF.Identity, scale=cms[:, NG + c:NG + c + 1], r=["R", "cms"], w=["Eb"])
                kv_update(kbh, 1, c)
            dma("sp", R[:, :, :], s0[0, h].rearrange("(a q) v -> q a v", q=128), w=["R"])
            for c in range(NG):
                csl = slice(c * 128, (c + 1) * 128)
                pi = c % 2
                act(Efc[pi][:, :], Rflat, AF.Identity, scale=cms[:, c:c + 1], r=["R", "cms"], w=[("Efc", pi)])
                tt(qfc[pi][:, :, :], qTh[:, :, csl], qd[:, 0, :].rearrange("p (a b) -> p a b", a=2), ALU.mult, r=["qTh", "qd"], w=[("qfc", pi)])
                tt(qbc[pi][:, :, :], qTh[:, :, csl], qd[:, 1, :].rearrange("p (a b) -> p a b", a=2), ALU.mult, r=["qTh", "qd"], w=[("qbc", pi)])
                b = nbank()
                for dc in range(2):
                    mm(ps[b][:, 0:128], kTh[:, dc, csl], qTh[:, dc, csl], dc == 0, dc == 1, r=["kTh", "qTh"], w=[("ps", b)])
                tt(PT[pi][:], ps[b][:, 0:128], MT[:, h, :], ALU.mult, r=[("ps", b), "MT"], w=[("PT", pi)])
                bo = nbank()
                mm(ps[bo][:, :], PT[pi][:], vh[:, c, :], True, False, r=[("PT", pi), "vh"], w=[("ps", bo)])
                for dc in range(2):
                    mm(ps[bo][:, :], qfc[pi][:, dc, :], Efc[pi][:, dc * 512:(dc + 1) * 512], False, False, r=[("qfc", pi), ("Efc", pi)], w=[("ps", bo)])
                for dc in range(2):
                    mm(ps[bo][:, :], qbc[pi][:, dc, :], Eb[:, c, dc * 512:(dc + 1) * 512], False, dc == 1, r=[("qbc", pi), "Eb"], w=[("ps", bo)])
                act(junk[:], ps[bo][:, :], AF.Square, accum_out=ssum[:, 0:1], r=[("ps", bo)], w=["ssum", "junk"])
                tsc(ssum[:, 1:2], ssum[:, 0:1], 1.0 / 512, EPS, ALU.mult, ALU.add, r=["ssum"], w=["ssum2"])
                rsqrt_(ssum[:, 1:2], "ssum2")
                stt(gob[:], ps[bo][:, :], ssum[:, 1:2], sgh[:, c, :], ALU.mult, ALU.mult, r=[("ps", bo), "ssum2", "sgh"], w=["gob"])
                tb = ntbank()
                for cc in range(4):
                    tr(pst[tb][:, cc * 128:(cc + 1) * 128], gob[:, cc * 128:(cc + 1) * 128], idb[:], r=["gob", "idb"], w=[("pst", tb)])
                S.add("act", lambda e, tb=tb, c=c: e.activation(out=goT[:, :, c * 128:(c + 1) * 128], in_=pst[tb][:, 0:512].rearrange("p (c t) -> p c t", c=4), func=AF.Copy),
                      [("pst", tb)], ["goT"])
                kv_update(kfh, 0, c)
            for cc in range(4):
                dma("sp", gT[(h * 4 + cc) * 128:(h * 4 + cc + 1) * 128, :], goT[:, cc, :], r=["goT"], w=["gT"])
    S.barrier()

    if getattr(cfg, "STOP", 99) <= 6:
        S.emit(); es0.close(); return nc, names_in
    out_proj(ret_wo, 2, MOD(1, 2), "r3")
    ffn(1)

    for p in range(2):
        with contextlib.ExitStack() as st:
            norm_mod(st, p * TP, TP, wm[:, 4 * NCD:5 * NCD], zshift, None, "yfin", out_dram=yT)
        S.barrier()

    S.emit()
    es0.close()
    return nc, names_in

BF = ml_dtypes.bfloat16


def ws_blocks(W):
    K, C = W.shape
    return np.ascontiguousarray(W.reshape(K // 128, 128, C // 128, 128).transpose(2, 1, 0, 3)).reshape(C // 128, 128, (K // 128) * 128)


def as_blocks(W, n):
    K, C = W.shape
    return np.ascontiguousarray(W.reshape(K // 128, 128, C // n, n).transpose(2, 1, 0, 3)).reshape(C // n, 128, (K // 128) * n)


def colT(v, n=None):
    v = np.asarray(v, np.float32)
    return np.ascontiguousarray(v.reshape(-1, 128).T)


def rep(v):
    return np.ascontiguousarray(np.broadcast_to(np.asarray(v, np.float32)[None, :], (128, len(v))))


def dft_tables(cfg, is_sample):
    TS, NG = cfg.TS, cfg.NG
    L = TS if is_sample else cfg.LP
    n = 2 * L
    nblk = TS // L
    j = np.arange(L)[:, None].astype(np.float64)
    f = np.arange(L)[None, :].astype(np.float64)
    ang = 2 * np.pi * j * f / n
    fre = np.cos(ang)
    fim = -np.sin(ang); fim[:, 0] = 0.0
    fn = np.zeros((L, L)); fn[:, 0] = np.cos(np.pi * np.arange(L))
    t = np.arange(L)[None, :].astype(np.float64)
    ff = np.arange(L)[:, None].astype(np.float64)
    ang2 = 2 * np.pi * ff * t / n
    gre = (2.0 / n) * np.cos(ang2); gre[0, :] = 1.0 / n
    gim = -(2.0 / n) * np.sin(ang2); gim[0, :] = (1.0 / n) * np.cos(np.pi * np.arange(L))

    def bd(m):
        out = np.zeros((TS, TS))
        for b in range(nblk):
            out[b * L:(b + 1) * L, b * L:(b + 1) * L] = m
        return out

    def fl(m):
        return np.ascontiguousarray(m.reshape(NG, 128, NG, 128).transpose(2, 1, 0, 3)).reshape(NG, 128, NG * 128).astype(BF)
    Fre, Fim, Fn_ = bd(fre), bd(fim), bd(fn)
    G = np.concatenate([bd(gre), bd(gim)], axis=0)
    Gm = np.ascontiguousarray(G.reshape(2 * NG, 128, NG, 128).transpose(2, 1, 0, 3)).reshape(NG, 128, 2 * NG * 128).astype(BF)
    sp = np.zeros(TS); sp[::L] = 1.0
    return dict(Fre=fl(Fre), Fim=fl(Fim), Fzi=fl(Fim + Fn_), Fn=fl(Fn_), Gm=Gm,
                spT=colT(sp), nsT=colT(1.0 - sp))


def host_inputs(cfg, inp):
    D, NCD, NH, TS, NG, NFF, LP = cfg.D, cfg.NCD, cfg.NH, cfg.TS, cfg.NG, cfg.NFF, cfg.LP
    f32 = lambda a: np.asarray(a, np.float32)
    shared = {}
    ada_w = f32(inp["ada_w"])
    shared["ada_wb"] = np.concatenate([ws_blocks(ada_w[l]) for l in range(2)], axis=0)
    shared["ada_bT"] = colT(f32(inp["ada_b"]).reshape(-1))
    shared["n1T"] = colT(f32(inp["norm1_w"]).reshape(-1))
    shared["n2T"] = colT(f32(inp["norm2_w"]).reshape(-1))
    shared["nfT"] = colT(f32(inp["final_norm_w"]))
    wb = ws_blocks(f32(inp["hy_w_in"])[0])
    shared["hy_win"] = np.ascontiguousarray(wb.reshape(3, NCD, 128, -1).transpose(1, 0, 2, 3)).reshape(3 * NCD, 128, -1)
    sw = f32(inp["hy_short_w"])[0]
    shared["hy_cw"] = np.concatenate([colT(sw[t]) for t in range(3)], axis=1)
    shared["hy_sb"] = colT(f32(inp["hy_short_b"])[0])
    shared["hy_w1"] = f32(inp["hy_f_w1"])[0]; shared["hy_w2"] = f32(inp["hy_f_w2"])[0]; shared["hy_w3"] = f32(inp["hy_f_w3"])[0]
    fr = f32(inp["hy_f_freq"])[0]
    shared["hy_bf"] = np.ascontiguousarray(np.stack([f32(inp["hy_f_b1"])[0], f32(inp["hy_f_b2"])[0], f32(inp["hy_f_b3"])[0], fr[0], fr[1], fr[2]], axis=1))
    shared["hy_wo"] = f32(inp["hy_f_wout"])[0]
    shared["hy_fb"] = rep(f32(inp["hy_f_bias"])[0])
    shared["hy_wout"] = ws_blocks(f32(inp["hy_w_out"])[0])
    deltas = np.abs(np.linspace(math.log(0.3) / 1e-2, math.log(1.5) / 1e-2, D, dtype=np.float32))
    shared["deltab"] = rep(deltas)
    for l in range(2):
        shared[f"ffn_up{l}"] = ws_blocks(f32(inp["ffn_w_up"])[l])
        cw = f32(inp["ffn_conv_w"])[l]
        shared[f"ffn_cw{l}"] = np.concatenate([colT(cw[t]) for t in range(3)], axis=1)
        wd = f32(inp["ffn_w_down"])[l]
        r0 = 0
        for e in range(cfg.NE):
            n = cfg.ESZ[e] * 128
            shared[f"ffn_dn{l}_{e}"] = ws_blocks(wd[r0:r0 + n])
            r0 += n
    rw = f32(inp["ret_w_in"])[0]
    qb_ = ws_blocks(rw[:, 0:D]); kb_ = ws_blocks(rw[:, D:2 * D])
    shared["ret_qk"] = np.ascontiguousarray(np.stack([qb_.reshape(NH, 2, 128, -1), kb_.reshape(NH, 2, 128, -1)], axis=1)).reshape(4 * NH, 128, -1)
    vb_ = as_blocks(rw[:, 2 * D:4 * D], 256).reshape(NH, 2, 128, -1); gb_ = as_blocks(rw[:, 4 * D:6 * D], 256).reshape(NH, 2, 128, -1)
    shared["ret_vg"] = np.ascontiguousarray(np.stack([vb_, gb_], axis=1)).reshape(4 * NH, 128, -1)
    wo = f32(inp["ret_w_out"])[0]
    shared["ret_wo"] = np.stack([ws_blocks(wo[0:D]), ws_blocks(wo[D:2 * D])], axis=0)
    shared["ret_lg"] = rep(f32(inp["ret_decay_logit"])[0].reshape(-1))
    shared["ret_gn"] = rep(f32(inp["ret_gn_w"])[0])
    i_ = np.arange(128)[None, :]; j_ = np.arange(128)[:, None]
    rt = [np.maximum(i_ - j_, 0), np.maximum(j_ - i_, 0), (i_ > j_), (j_ > i_), 2.0 * np.eye(128),
          np.broadcast_to(i_ + 1, (128, 128)), np.broadcast_to(128 - i_, (128, 128))]
    shared["rtab"] = np.ascontiguousarray(np.concatenate([np.asarray(a, np.float32) for a in rt], axis=1))
    p_ = np.arange(128, dtype=np.float32)[:, None]
    shared["kpos"] = np.ascontiguousarray(np.concatenate([np.broadcast_to(127 - p_, (128, NH)), np.broadcast_to(p_, (128, NH))], axis=1))
    shared["identb"] = np.eye(128, dtype=np.float32).astype(BF)

    def span_tables(is_sample):
        d = dft_tables(cfg, is_sample)
        L = TS if is_sample else LP
        pos = np.arange(TS) % L
        d["mS"] = rep((pos != 0).astype(np.float32))
        d["mE"] = rep((pos != L - 1).astype(np.float32))
        t = np.linspace(0.0, 1.0, L, dtype=np.float32)
        w = 2.0 * np.pi * np.arange(L, dtype=np.float32) / L
        f = np.linspace(1e-4, 15.0, 16, dtype=np.float32)[None, :]
        z = np.concatenate([t[:, None], np.cos(f * w[:, None]), -np.sin(f * w[:, None])], axis=-1).astype(np.float32)
        d["zembT"] = np.ascontiguousarray(np.tile(z, (TS // L, 1)).T)
        d["negt"] = colT(-np.tile(t, TS // L))
        d["nfirst"] = colT((pos != 0).astype(np.float32))
        if is_sample:
            GW = 64
            rows = TS // GW
            row = np.repeat(np.arange(rows, dtype=np.float32), GW); col = np.tile(np.arange(GW, dtype=np.float32), rows)
            inv = (10000.0 ** (-np.arange(64, dtype=np.float32) / 64)).astype(np.float32)
            ang = np.concatenate([row[:, None] * inv, col[:, None] * inv], axis=-1)
            d["cosq"] = np.ascontiguousarray(np.cos(ang).T.astype(np.float32)); d["sinq"] = np.ascontiguousarray(np.sin(ang).T.astype(np.float32))
            d["cmf"] = rep(np.ones(NG)); d["cmb"] = rep(np.ones(NG))
        else:
            d["cosq"] = np.ones((128, TS), np.float32); d["sinq"] = np.zeros((128, TS), np.float32)
            c = np.arange(NG)
            cmf = (c % cfg.CPS != 0).astype(np.float32); cmf[0] = 1.0
            cmb = (c % cfg.CPS != cfg.CPS - 1).astype(np.float32); cmb[NG - 1] = 1.0
            d["cmf"] = rep(cmf); d["cmb"] = rep(cmb)
        return d
    tab_s, tab_p = span_tables(True), span_tables(False)
    maps = []
    xs, xp = f32(inp["x_sample"]), f32(inp["x_prompt"])
    for core in range(cfg.NCORES):
        m = dict(shared)
        if core < 2:
            m.update(tab_s)
            m["xT0"] = np.ascontiguousarray(xs[core].T)
            m["cvec"] = colT(f32(inp["c"])[core])
            m["s0"] = np.ascontiguousarray(f32(inp["state_retention"])[core, 0])
        else:
            m.update(tab_p)
            b0 = (core - 2) * cfg.NPC
            m["xT0"] = np.ascontiguousarray(xp[b0:b0 + cfg.NPC].reshape(TS, D).T)
            m["cvec"] = colT(f32(inp["c_ctx"]))
            m["s0"] = np.zeros((2, NH, 256, 512), np.float32)
        maps.append(m)
    return maps


_CACHE = {}


def run(cfg, inp, trace=False):
    key = id(cfg)
    if key not in _CACHE:
        _CACHE[key] = build(cfg)
    nc, names = _CACHE[key]
    maps = host_inputs(cfg, inp)
    maps = [{k: m[k] for k in names} for m in maps]
    res = run_bass_kernel_spmd(nc, maps, core_ids=list(range(cfg.NCORES)), trace=trace)
    D, TS, NH = cfg.D, cfg.TS, cfg.NH
    outs = res.results
    y_sample = np.stack([np.ascontiguousarray(outs[c]["yT"].T) for c in range(2)], axis=0)
    y_prompt = np.concatenate([np.ascontiguousarray(outs[c]["yT"].T).reshape(cfg.NPC, cfg.LP, D) for c in range(2, cfg.NCORES)], axis=0)
    st = np.concatenate([outs[c]["st_out"].transpose(2, 0, 1, 3, 4) for c in range(2, cfg.NCORES)], axis=0)[:, None]
    return (y_prompt.astype(np.float32), y_sample.astype(np.float32), np.ascontiguousarray(st).astype(np.float32)), res


def kernel(**inputs):
    out, _ = run(FULL, inputs)
    return out
```
